# Optimizing a Trainium2 kernel written in Bass

```python
import math
import jax, jax.numpy as jnp
from jax import lax
import numpy as np

D_MODEL = 4096
BATCH = 2
SEQ = 8192
DEPTH = 1

NSA_HEADS = 32
NSA_KV_GROUPS = 4
NSA_HEAD_DIM = 128
NSA_CMP_BLOCK = 32
NSA_CMP_STRIDE = 16
NSA_CMP_HIDDEN = 256
NSA_SEL_BLOCK = 64
NSA_SEL_TOPN = 16
NSA_WINDOW = 512
NSA_Q_BLOCK = 64
ROPE_THETA = 500000.0
ROPE_DIM = NSA_HEAD_DIM // 4
FORCE_BONUS = 1.0e4
NEG_INF = -1.0e30

SSD_D_INNER = D_MODEL
SSD_HEAD_DIM = 64
SSD_HEADS = SSD_D_INNER // SSD_HEAD_DIM
SSD_GROUPS = 8
SSD_STATE = 128
SSD_CONV = 4
SSD_CHUNK = 256
SSD_CONV_CH = SSD_D_INNER + 2 * SSD_GROUPS * SSD_STATE

FFN_DIM = 256 * ((8 * D_MODEL // 3 + 255) // 256)
N_ADA = 9
ADA_SCALE = 0.5
LN_EPS = 1e-5
RMS_EPS = 1e-5
DEEPNORM_ALPHA = (2 * DEPTH) ** 0.25
DEEPNORM_BETA = (8 * DEPTH) ** -0.25
POS_OFFSET_MAX = 4096

NSA_Q_WIDTH = NSA_HEADS * NSA_HEAD_DIM
NSA_KV_WIDTH = NSA_KV_GROUPS * NSA_HEAD_DIM
IN_SPLIT_SIZES = (NSA_Q_WIDTH,) + (NSA_KV_WIDTH,) * 6 + (3 * NSA_HEADS, SSD_D_INNER, SSD_CONV_CH, SSD_HEADS, D_MODEL, D_MODEL)
IN_PROJ_DIM = sum(IN_SPLIT_SIZES)

kernel_name = "hybrid_nsa_ssd_macaron_deepnorm_adaln"


def layer_norm(x, g, b):
    xf = x.astype(jnp.float32)
    mu = jnp.mean(xf, axis=-1, keepdims=True)
    var = jnp.mean(jnp.square(xf - mu), axis=-1, keepdims=True)
    return ((xf - mu) * lax.rsqrt(var + LN_EPS) * g.astype(jnp.float32) + b.astype(jnp.float32)).astype(x.dtype)


def swiglu(h, w_gate, w_up, w_down):
    return (jax.nn.silu(h @ w_gate) * (h @ w_up)) @ w_down


def modulate(x, shift, scale):
    return x * (1.0 + scale[:, None, :]) + shift[:, None, :]


def rope_tables(pos):
    half = ROPE_DIM // 2
    inv_freq = jnp.float32(ROPE_THETA) ** (-jnp.arange(half, dtype=jnp.float32) / half)
    ang = pos.astype(jnp.float32)[..., None] * inv_freq
    return jnp.cos(ang), jnp.sin(ang)


def apply_partial_rope(t, cos, sin):
    half = ROPE_DIM // 2
    cs = cos[:, :, None, :].astype(t.dtype)
    sn = sin[:, :, None, :].astype(t.dtype)
    t1 = t[..., :half]
    t2 = t[..., half:ROPE_DIM]
    return jnp.concatenate([t1 * cs - t2 * sn, t2 * cs + t1 * sn, t[..., ROPE_DIM:]], axis=-1)


def masked_softmax(s, mask):
    p = jax.nn.softmax(jnp.where(mask, s, NEG_INF), axis=-1)
    return jnp.where(mask, p, 0.0)


def nsa_compress(kv, pos_emb, w1, w2):
    B, S, G, dh = kv.shape
    n_cmp = (S - NSA_CMP_BLOCK) // NSA_CMP_STRIDE + 1
    idx = NSA_CMP_STRIDE * np.arange(n_cmp)[:, None] + np.arange(NSA_CMP_BLOCK)[None, :]
    blocks = kv[:, idx] + pos_emb[None, None, :, None, :]
    blocks = jnp.swapaxes(blocks, 2, 3).reshape(B, n_cmp, G, NSA_CMP_BLOCK * dh)
    return jax.nn.silu(blocks @ w1) @ w2


def nsa_attention(q, k_c, v_c, k_s, v_s, k_w, v_w, g_c, g_s, g_w):
    B, S, H, dh = q.shape
    G = NSA_KV_GROUPS
    hpg = H // G
    n_cmp = k_c.shape[1]
    n_sel = S // NSA_SEL_BLOCK
    top_n = min(NSA_SEL_TOPN, n_sel)
    Tq = NSA_Q_BLOCK
    n_qb = S // Tq
    W = NSA_WINDOW
    Bl = NSA_SEL_BLOCK
    scale = NSA_HEAD_DIM ** -0.5
    f32 = jnp.float32

    qg = q.reshape(B, S, G, hpg, dh)
    gc = g_c.reshape(B, S, G, hpg, 1).astype(q.dtype)
    gs = g_s.reshape(B, S, G, hpg, 1).astype(q.dtype)
    gw = g_w.reshape(B, S, G, hpg, 1).astype(q.dtype)

    c_start = jnp.arange(n_cmp) * NSA_CMP_STRIDE
    c_end = c_start + NSA_CMP_BLOCK - 1
    sel_start = jnp.arange(n_sel) * Bl
    overlap = ((c_start[:, None] < sel_start[None, :] + Bl) & (c_end[:, None] >= sel_start[None, :])).astype(f32)
    j_idx = jnp.arange(n_sel)

    ks_blocks = k_s.reshape(B, n_sel, Bl, G, dh).transpose(0, 3, 1, 2, 4)
    vs_blocks = v_s.reshape(B, n_sel, Bl, G, dh).transpose(0, 3, 1, 2, 4)
    pad = ((0, 0), (W, 0), (0, 0), (0, 0))
    kw_pad = jnp.pad(k_w, pad)
    vw_pad = jnp.pad(v_w, pad)
    b_ix = jnp.arange(B)[:, None, None, None]
    g_ix = jnp.arange(G)[None, :, None, None]

    def block_fn(i):
        t0 = i * Tq
        t = t0 + jnp.arange(Tq)
        qb = lax.dynamic_slice_in_dim(qg, t0, Tq, axis=1)

        s_c = jnp.einsum('bqghd,bcgd->bghqc', qb, k_c, preferred_element_type=f32) * scale
        m_c = c_end[None, :] <= t[:, None]
        p_c = masked_softmax(s_c, m_c)
        o_c = jnp.einsum('bghqc,bcgd->bqghd', p_c.astype(v_c.dtype), v_c)

        imp = jnp.einsum('bghqc,cj->bgqj', p_c, overlap)
        cur = t // Bl
        causal_blk = sel_start[None, :] <= t[:, None]
        forced = (j_idx[None, :] == 0) | (j_idx[None, :] == cur[:, None]) | (j_idx[None, :] == cur[:, None] - 1)
        score = jnp.where(causal_blk, imp + jnp.where(forced, FORCE_BONUS, 0.0), NEG_INF)
        _, sel = lax.top_k(score, top_n)
        k_sel = ks_blocks[b_ix, g_ix, sel].reshape(B, G, Tq, top_n * Bl, dh)
        v_sel = vs_blocks[b_ix, g_ix, sel].reshape(B, G, Tq, top_n * Bl, dh)
        key_pos = (sel[..., None] * Bl + jnp.arange(Bl)).reshape(B, G, Tq, top_n * Bl)
        m_s = (key_pos <= t[None, None, :, None])[:, :, None]
        s_s = jnp.einsum('bqghd,bgqkd->bghqk', qb, k_sel, preferred_element_type=f32) * scale
        p_s = masked_softmax(s_s, m_s)
        o_s = jnp.einsum('bghqk,bgqkd->bqghd', p_s.astype(v_sel.dtype), v_sel)

        kwb = lax.dynamic_slice_in_dim(kw_pad, t0, Tq + W, axis=1)
        vwb = lax.dynamic_slice_in_dim(vw_pad, t0, Tq + W, axis=1)
        kpos = t0 - W + jnp.arange(Tq + W)
        m_w = (kpos[None, :] <= t[:, None]) & (kpos[None, :] > t[:, None] - W) & (kpos[None, :] >= 0)
        s_w = jnp.einsum('bqghd,bkgd->bghqk', qb, kwb, preferred_element_type=f32) * scale
        p_w = masked_softmax(s_w, m_w)
        o_w = jnp.einsum('bghqk,bkgd->bqghd', p_w.astype(vwb.dtype), vwb)

        gcb = lax.dynamic_slice_in_dim(gc, t0, Tq, axis=1)
        gsb = lax.dynamic_slice_in_dim(gs, t0, Tq, axis=1)
        gwb = lax.dynamic_slice_in_dim(gw, t0, Tq, axis=1)
        o = gcb * o_c + gsb * o_s + gwb * o_w
        return o.reshape(B, Tq, H * dh)

    out = lax.map(block_fn, jnp.arange(n_qb))
    return jnp.swapaxes(out, 0, 1).reshape(B, S, H * dh)


def causal_depthwise_conv(x, w, b):
    C = x.shape[-1]
    out = lax.conv_general_dilated(x, w[:, None, :].astype(x.dtype), window_strides=(1,),
                                   padding=[(SSD_CONV - 1, 0)], dimension_numbers=('NWC', 'WIO', 'NWC'),
                                   feature_group_count=C)
    return out + b.astype(x.dtype)


def ssd_mixer(z, xbc, dt_raw, conv_w, conv_b, dt_bias, a_log, d_skip, norm_w):
    B, S, _ = z.shape
    G, N, P = SSD_GROUPS, SSD_STATE, SSD_HEAD_DIM
    hpg = SSD_HEADS // G
    f32 = jnp.float32
    xbc = jax.nn.silu(causal_depthwise_conv(xbc, conv_w, conv_b))
    xs = xbc[..., :SSD_D_INNER].reshape(B, S, G, hpg, P)
    b_in = xbc[..., SSD_D_INNER:SSD_D_INNER + G * N].reshape(B, S, G, N).astype(f32)
    c_in = xbc[..., SSD_D_INNER + G * N:].reshape(B, S, G, N).astype(f32)
    dt = jax.nn.softplus(dt_raw.astype(f32) + dt_bias.astype(f32))
    a = -jnp.exp(a_log.astype(f32))
    da = (dt * a).reshape(B, S, G, hpg)
    xdt = xs.astype(f32) * dt.reshape(B, S, G, hpg, 1)

    L = math.gcd(S, SSD_CHUNK)
    nc = S // L

    def to_chunks(t):
        return jnp.moveaxis(t.reshape((B, nc, L) + t.shape[2:]), 1, 0)

    causal = jnp.tril(jnp.ones((L, L), dtype=bool))[None, :, :, None, None]

    def step(state, inp):
        xc, dac, bc, cc = inp
        a_cum = jnp.cumsum(dac, axis=1)
        diff = a_cum[:, :, None] - a_cum[:, None, :]
        decay = jnp.exp(jnp.where(causal, diff, -jnp.inf))
        cb = jnp.einsum('btgn,bsgn->btsg', cc, bc)
        y_intra = jnp.einsum('btsg,btsgh,bsghp->btghp', cb, decay, xc)
        y_inter = jnp.einsum('btgn,bghpn->btghp', cc, state) * jnp.exp(a_cum)[..., None]
        decay_end = jnp.exp(a_cum[:, -1:] - a_cum)
        new_state = state * jnp.exp(a_cum[:, -1])[..., None, None] + jnp.einsum('bsgn,bsgh,bsghp->bghpn', bc, decay_end, xc)
        return new_state, y_intra + y_inter

    state0 = jnp.zeros((B, G, hpg, P, N), f32)
    _, ys = lax.scan(step, state0, (to_chunks(xdt), to_chunks(da), to_chunks(b_in), to_chunks(c_in)))
    y = jnp.moveaxis(ys, 0, 1).reshape(B, S, G, hpg, P)
    y = y + d_skip.astype(f32).reshape(G, hpg, 1) * xs.astype(f32)
    yg = (y.reshape(B, S, SSD_D_INNER) * jax.nn.silu(z.astype(f32))).reshape(B, S, G, -1)
    yg = yg * lax.rsqrt(jnp.mean(jnp.square(yg), axis=-1, keepdims=True) + RMS_EPS)
    return (yg.reshape(B, S, SSD_D_INNER) * norm_w.astype(f32)).astype(z.dtype)


def hybrid_mixer(h, positions, cos, sin, w_in, cmp_pos, cmp_k_w1, cmp_k_w2, cmp_v_w1, cmp_v_w2,
                 conv_w, conv_b, dt_bias, a_log, d_skip, norm_w, w_branch_a, w_branch_b, w_out):
    B, S, _ = h.shape
    H, G, dh = NSA_HEADS, NSA_KV_GROUPS, NSA_HEAD_DIM
    split_points = [int(v) for v in np.cumsum(IN_SPLIT_SIZES)[:-1]]
    proj = h @ w_in
    (q, k_cmp, v_cmp, k_slc, v_slc, k_win, v_win, g_nsa,
     z, xbc, dt_raw, gate_a, gate_b) = jnp.split(proj, split_points, axis=-1)

    q = apply_partial_rope(q.reshape(B, S, H, dh), cos, sin)
    k_slc = apply_partial_rope(k_slc.reshape(B, S, G, dh), cos, sin)
    k_win = apply_partial_rope(k_win.reshape(B, S, G, dh), cos, sin)
    v_slc = v_slc.reshape(B, S, G, dh)
    v_win = v_win.reshape(B, S, G, dh)
    k_c = nsa_compress(k_cmp.reshape(B, S, G, dh), cmp_pos, cmp_k_w1, cmp_k_w2)
    v_c = nsa_compress(v_cmp.reshape(B, S, G, dh), cmp_pos, cmp_v_w1, cmp_v_w2)
    n_cmp = k_c.shape[1]
    c_end_idx = NSA_CMP_STRIDE * np.arange(n_cmp) + NSA_CMP_BLOCK - 1
    cos_c, sin_c = rope_tables(positions[:, c_end_idx])
    k_c = apply_partial_rope(k_c, cos_c, sin_c)
    g3 = jax.nn.sigmoid(g_nsa.astype(jnp.float32)).reshape(B, S, 3, H)
    o_a = nsa_attention(q, k_c, v_c, k_slc, v_slc, k_win, v_win, g3[:, :, 0], g3[:, :, 1], g3[:, :, 2])

    o_b = ssd_mixer(z, xbc, dt_raw, conv_w, conv_b, dt_bias, a_log, d_skip, norm_w)

    y_a = o_a @ w_branch_a
    y_b = o_b @ w_branch_b
    merged = jax.nn.sigmoid(gate_a) * y_a + jax.nn.sigmoid(gate_b) * y_b
    return merged @ w_out


def setup_inputs(seed: int = 0) -> dict:
    key = jax.random.key(seed)
    ks = jax.random.split(key, 40)
    L, D, F = DEPTH, D_MODEL, FFN_DIM
    f32 = jnp.float32

    def nrm(k, shape, scale):
        return jax.random.normal(k, shape, f32) * scale

    x = nrm(ks[0], (BATCH, SEQ, D), 1.0)
    c = nrm(ks[1], (BATCH, D), 1.0)
    positions = (jnp.arange(SEQ, dtype=jnp.int32)[None, :]
                 + jax.random.randint(ks[2], (BATCH, 1), 0, POS_OFFSET_MAX, dtype=jnp.int32))
    w_ada = nrm(ks[3], (L, D, N_ADA * D), ADA_SCALE * D ** -0.5)
    b_ada = nrm(ks[4], (L, N_ADA * D), 0.02)
    ffn1_w_gate = nrm(ks[5], (L, D, F), D ** -0.5)
    ffn1_w_up = nrm(ks[6], (L, D, F), D ** -0.5)
    ffn1_w_down = nrm(ks[7], (L, F, D), DEEPNORM_BETA * F ** -0.5)
    w_in = nrm(ks[8], (L, D, IN_PROJ_DIM), D ** -0.5)
    nsa_cmp_pos = nrm(ks[9], (L, NSA_CMP_BLOCK, NSA_HEAD_DIM), 0.1)
    fan_c = NSA_CMP_BLOCK * NSA_HEAD_DIM
    nsa_cmp_k_w1 = nrm(ks[10], (L, fan_c, NSA_CMP_HIDDEN), fan_c ** -0.5)
    nsa_cmp_k_w2 = nrm(ks[11], (L, NSA_CMP_HIDDEN, NSA_HEAD_DIM), NSA_CMP_HIDDEN ** -0.5)
    nsa_cmp_v_w1 = nrm(ks[12], (L, fan_c, NSA_CMP_HIDDEN), fan_c ** -0.5)
    nsa_cmp_v_w2 = nrm(ks[13], (L, NSA_CMP_HIDDEN, NSA_HEAD_DIM), NSA_CMP_HIDDEN ** -0.5)
    ssd_conv_w = nrm(ks[14], (L, SSD_CONV, SSD_CONV_CH), SSD_CONV ** -0.5)
    ssd_conv_b = nrm(ks[15], (L, SSD_CONV_CH), 0.02)
    dt0 = jnp.exp(jax.random.uniform(ks[16], (L, SSD_HEADS), f32, math.log(1e-3), math.log(1e-1)))
    ssd_dt_bias = dt0 + jnp.log(-jnp.expm1(-dt0))
    ssd_a_log = jnp.log(jax.random.uniform(ks[17], (L, SSD_HEADS), f32, 1.0, 16.0))
    ssd_d = 1.0 + nrm(ks[18], (L, SSD_HEADS), 0.02)
    ssd_norm_w = 1.0 + nrm(ks[19], (L, SSD_D_INNER), 0.02)
    w_branch_a = nrm(ks[20], (L, NSA_Q_WIDTH, D), NSA_Q_WIDTH ** -0.5)
    w_branch_b = nrm(ks[21], (L, SSD_D_INNER, D), SSD_D_INNER ** -0.5)
    w_out = nrm(ks[22], (L, D, D), DEEPNORM_BETA * D ** -0.5)
    ffn2_w_gate = nrm(ks[23], (L, D, F), D ** -0.5)
    ffn2_w_up = nrm(ks[24], (L, D, F), D ** -0.5)
    ffn2_w_down = nrm(ks[25], (L, F, D), DEEPNORM_BETA * F ** -0.5)
    ln1_g = 1.0 + nrm(ks[26], (L, D), 0.02)
    ln1_b = nrm(ks[27], (L, D), 0.02)
    ln2_g = 1.0 + nrm(ks[28], (L, D), 0.02)
    ln2_b = nrm(ks[29], (L, D), 0.02)
    ln3_g = 1.0 + nrm(ks[30], (L, D), 0.02)
    ln3_b = nrm(ks[31], (L, D), 0.02)
    return {"x": x, "c": c, "positions": positions, "w_ada": w_ada, "b_ada": b_ada,
            "ffn1_w_gate": ffn1_w_gate, "ffn1_w_up": ffn1_w_up, "ffn1_w_down": ffn1_w_down,
            "w_in": w_in, "nsa_cmp_pos": nsa_cmp_pos, "nsa_cmp_k_w1": nsa_cmp_k_w1, "nsa_cmp_k_w2": nsa_cmp_k_w2,
            "nsa_cmp_v_w1": nsa_cmp_v_w1, "nsa_cmp_v_w2": nsa_cmp_v_w2,
            "ssd_conv_w": ssd_conv_w, "ssd_conv_b": ssd_conv_b, "ssd_dt_bias": ssd_dt_bias,
            "ssd_a_log": ssd_a_log, "ssd_d": ssd_d, "ssd_norm_w": ssd_norm_w,
            "w_branch_a": w_branch_a, "w_branch_b": w_branch_b, "w_out": w_out,
            "ffn2_w_gate": ffn2_w_gate, "ffn2_w_up": ffn2_w_up, "ffn2_w_down": ffn2_w_down,
            "ln1_g": ln1_g, "ln1_b": ln1_b, "ln2_g": ln2_g, "ln2_b": ln2_b, "ln3_g": ln3_g, "ln3_b": ln3_b}


def reference(x, c, positions, w_ada, b_ada, ffn1_w_gate, ffn1_w_up, ffn1_w_down, w_in, nsa_cmp_pos,
              nsa_cmp_k_w1, nsa_cmp_k_w2, nsa_cmp_v_w1, nsa_cmp_v_w2, ssd_conv_w, ssd_conv_b, ssd_dt_bias,
              ssd_a_log, ssd_d, ssd_norm_w, w_branch_a, w_branch_b, w_out, ffn2_w_gate, ffn2_w_up, ffn2_w_down,
              ln1_g, ln1_b, ln2_g, ln2_b, ln3_g, ln3_b):
    B = x.shape[0]
    cos, sin = rope_tables(positions)
    c_act = jax.nn.silu(c)
    for l in range(DEPTH):
        mod = (c_act @ w_ada[l] + b_ada[l]).reshape(B, N_ADA, D_MODEL)
        h = modulate(x, mod[:, 0], mod[:, 1])
        y = swiglu(h, ffn1_w_gate[l], ffn1_w_up[l], ffn1_w_down[l])
        x = layer_norm(DEEPNORM_ALPHA * x + 0.5 * mod[:, 2, None, :] * y, ln1_g[l], ln1_b[l])
        h = modulate(x, mod[:, 3], mod[:, 4])
        y = hybrid_mixer(h, positions, cos, sin, w_in[l], nsa_cmp_pos[l], nsa_cmp_k_w1[l], nsa_cmp_k_w2[l],
                         nsa_cmp_v_w1[l], nsa_cmp_v_w2[l], ssd_conv_w[l], ssd_conv_b[l], ssd_dt_bias[l],
                         ssd_a_log[l], ssd_d[l], ssd_norm_w[l], w_branch_a[l], w_branch_b[l], w_out[l])
        x = layer_norm(DEEPNORM_ALPHA * x + mod[:, 5, None, :] * y, ln2_g[l], ln2_b[l])
        h = modulate(x, mod[:, 6], mod[:, 7])
        y = swiglu(h, ffn2_w_gate[l], ffn2_w_up[l], ffn2_w_down[l])
        x = layer_norm(DEEPNORM_ALPHA * x + 0.5 * mod[:, 8, None, :] * y, ln3_g[l], ln3_b[l])
    return x
```

```python
import numpy as np
from contextlib import ExitStack
import concourse.bass as bass
import concourse.mybir as mybir
from concourse.bass_utils import run_bass_kernel_spmd

F32 = mybir.dt.float32
BF16 = mybir.dt.bfloat16
I32 = mybir.dt.int32
AF = mybir.ActivationFunctionType
ALU = mybir.AluOpType

D = 4096
DC = D // 128
FF = 11008
FC = FF // 128
N_ADA = 9
LN_EPS = 1e-5
ALPHA = 2.0 ** 0.25
TT = 512


class Buf:
    __slots__ = ("w", "r", "name")

    def __init__(self, name=""):
        self.w = None
        self.r = {}
        self.name = name


class Sched:
    RING = 12

    def __init__(self, nc, es):
        self.nc = nc
        self.eng = dict(pe=nc.tensor, act=nc.scalar, dve=nc.vector, pool=nc.gpsimd, sp=nc.sync)
        self.sem = {k: es.enter_context(nc.semaphore("s_" + k)) for k in ("pe", "act", "dve", "pool")}
        self.cnt = dict.fromkeys(self.sem, 0)
        self.known = {k: {} for k in self.eng}
        self.rings = {q: [es.enter_context(nc.semaphore("d_%s%d" % (q, i))) for i in range(self.RING)]
                      for q in ("sp", "pool", "act")}
        self.ring_j = dict.fromkeys(self.rings, 0)
        self.ring_tok = {q: [None] * self.RING for q in self.rings}
        self.nwait = 0

    def _wait(self, e, tok):
        sem, val, key = tok
        if e == "pe" and key == "pe":
            return
        k = self.known[e]
        if k.get(key, 0) >= val:
            return
        self.eng[e].wait_ge(sem, val)
        self.nwait += 1
        k[key] = val

    def _deps(self, reads, writes):
        need = []
        for b in reads:
            if b.w is not None:
                need.append(b.w)
        for b in writes:
            if b.w is not None:
                need.append(b.w)
            need.extend(b.r.values())
        return need

    def _mark(self, tok, reads, writes):
        for b in reads:
            b.r[tok[2]] = tok
        for b in writes:
            b.w = tok
            b.r = {}

    def op(self, e, fn, reads=(), writes=()):
        for t in self._deps(reads, writes):
            self._wait(e, t)
        ins = fn(self.eng[e])
        self.cnt[e] += 1
        tok = (self.sem[e], self.cnt[e], e)
        ins.then_inc(self.sem[e], 1)
        self._mark(tok, reads, writes)
        return tok

    def dma(self, q, out, in_, reads=(), writes=(), **kw):
        need = self._deps(reads, writes)
        j = self.ring_j[q]
        slot = j % self.RING
        prev = self.ring_tok[q][slot]
        if prev is not None:
            need.append(prev)
        for t in need:
            self._wait(q, t)
        sem = self.rings[q][slot]
        val = 16 * (j // self.RING + 1)
        self.eng[q].dma_start(out=out, in_=in_, **kw).then_inc(sem, 16)
        tok = (sem, val, (q, slot))
        self.ring_j[q] = j + 1
        self.ring_tok[q][slot] = tok
        self._mark(tok, reads, writes)
        return tok

    def barrier(self):
        toks = [(self.sem[k], self.cnt[k], k) for k in self.sem if self.cnt[k] > 0]
        for q in self.rings:
            toks.extend(t for t in self.ring_tok[q] if t is not None)
        for e in self.eng:
            for t in toks:
                if t[2] == e and e != "pe":
                    pass
                sem, val, key = t
                k = self.known[e]
                if k.get(key, 0) >= val:
                    continue
                self.eng[e].wait_ge(sem, val)
                k[key] = val


class Ctx:
    def __init__(self, nc, es):
        self.nc = nc
        self.es = es
        self.S = Sched(nc, es)
        self.banks = []
        for i in range(8):
            t = es.enter_context(nc.psum_tensor("bank%d" % i, [128, 512], F32))
            self.banks.append((t, Buf("bank%d" % i)))
        self.bank_i = 0

    def bank(self, hold=False):
        held = getattr(self, "held", None)
        if held is None:
            held = self.held = set()
        for _ in range(8):
            i = self.bank_i % 8
            self.bank_i += 1
            if i not in held:
                if hold:
                    held.add(i)
                return self.banks[i]
        raise RuntimeError("all PSUM banks held")

    def release(self, *bs):
        for b in bs:
            for i, bb in enumerate(self.banks):
                if bb[1] is b[1]:
                    self.held.discard(i)

    def sb(self, es, name, shape, dt):
        self.uid = getattr(self, "uid", 0) + 1
        return es.enter_context(self.nc.sbuf_tensor("%s_%d" % (name, self.uid), shape, dt))

    def scope(self):
        return Scope(self)


class Scope:
    def __init__(self, cx):
        self.cx = cx
        self.es = ExitStack()

    def __enter__(self):
        self.es.__enter__()
        return self.es

    def __exit__(self, *a):
        self.cx.S.barrier()
        return self.es.__exit__(*a)


def load_vec_fm(cx, es, name, vec_ap, n, ident):
    nc, S = cx.nc, cx.S
    c = n // 128
    out = cx.sb(es, name, [128, c], F32)
    ob = Buf(name)
    v2 = vec_ap.rearrange("(c p) -> c p", p=128)
    with cx.scope() as tes:
        r0 = 0
        k = 0
        while r0 < c:
            rows = min(96, c - r0)
            tmp = cx.sb(tes, "%s_t%d" % (name, k), [rows, 128], F32)
            tb = Buf()
            S.dma("sp", tmp[:], v2[r0:r0 + rows, :], writes=[tb])
            pt, pb = cx.bank()
            S.op("pe", lambda e: e.transpose(pt[:, 0:rows], tmp[:], ident[0:rows, 0:rows]), reads=[tb], writes=[pb])
            S.op("dve", lambda e: e.tensor_copy(out[:, r0:r0 + rows], pt[:, 0:rows]), reads=[pb], writes=[ob])
            r0 += rows
            k += 1
    return out, ob


def stage_ada(cx, es, c_ap, w_ada, b_ada, ident):
    nc, S = cx.nc, cx.S
    NCOL = N_ADA * DC
    modT = cx.sb(es, "modT", [128, NCOL], F32)
    modb = Buf("modT")
    with cx.scope() as tes:
        cT, cb = load_vec_fm(cx, tes, "cT", c_ap, D, ident)
        bT, bb = load_vec_fm(cx, tes, "bT", b_ada, N_ADA * D, ident)
        S.op("act", lambda e: e.activation(out=cT[:], in_=cT[:], func=AF.Silu), reads=[cb], writes=[cb])
        NB = 4
        wbuf = [cx.sb(tes, "wada%d" % i, [128, 4, 512], F32) for i in range(NB)]
        wb = [Buf() for _ in range(NB)]
        wi = 0
        pbanks = [cx.bank() for _ in range(4)]
        for ng in range(N_ADA * D // 512):
            tiles = []
            for k4 in range(DC // 4):
                t, b = wbuf[wi % NB], wb[wi % NB]
                wi += 1
                src = w_ada[k4 * 512:(k4 + 1) * 512, ng * 512:(ng + 1) * 512].rearrange("(a p) n -> p a n", p=128)
                S.dma("sp", t[:], src, writes=[b])
                for a in range(4):
                    kc = k4 * 4 + a
                    for j in range(4):
                        pt, pb = pbanks[j]
                        S.op("pe", lambda e: e.matmul(pt[:, ng:ng + 1], lhsT=t[:, a, j * 128:(j + 1) * 128],
                                                      rhs=cT[:, kc:kc + 1], start=(kc == 0), stop=(kc == DC - 1)),
                             reads=[b, cb], writes=[pb])
        mv = modT[:].rearrange("p (g j) -> p g j", j=4)
        bv = bT[:].rearrange("p (g j) -> p g j", j=4)
        for j in range(4):
            pt, pb = pbanks[j]
            S.op("dve", lambda e: e.tensor_tensor(out=mv[:, :, j], in0=pt[:, 0:NCOL // 4], in1=bv[:, :, j], op=ALU.add),
                 reads=[pb, bb], writes=[modb])
    return modT, modb


def ffn_stage(cx, name, S_len, xT_src, xsrc_bufs, hT_src, hsrc_bufs, modT, modb, sc_col, sh_col, Wg, Wu, Wd,
              gs, gsb, lng, lnb, lnbuf, nsc_col, nsh_col, xT_dst, xdst_bufs, hT_dst, hdst_bufs, ones, onesb, merge=None):
    nc, S = cx.nc, cx.S
    NT = S_len // TT
    xs_v = xT_src.rearrange("(c p) t -> p c t", p=128)
    xd_v = xT_dst.rearrange("(c p) t -> p c t", p=128)
    hs_v = hT_src.rearrange("(c p) t -> p c t", p=128) if hT_src is not None else None
    hd_v = hT_dst.rearrange("(c p) t -> p c t", p=128) if hT_dst is not None else None
    fgroups = [(f0, min(4, FC - f0)) for f0 in range(0, FC, 4)]
    passes = [fgroups[i:i + 6] for i in range(0, len(fgroups), 6)]
    APASS = 24
    if merge is not None:
        passes = [[(f0, 4) for f0 in range(0, DC, 4)]]
        APASS = DC
        oa_v = merge["oaT"].rearrange("(c p) t -> p c t", p=128)
        ob_v = merge["obT"].rearrange("(c p) t -> p c t", p=128)
    with cx.scope() as es:
        z = cx.sb(es, "z", [128, DC, TT], F32)
        zb = [Buf() for _ in range(DC)]
        h = cx.sb(es, "h", [128, DC, TT], BF16)
        hb = [Buf() for _ in range(DC)]
        a_sb = cx.sb(es, "a", [128, APASS, TT], BF16)
        ab = [Buf() for _ in range(APASS)]
        NST, NBF = (2, 3) if merge is not None else (3, 4)
        wst = [cx.sb(es, "wst%d" % i, [128, 2, 512], F32) for i in range(NST)]
        wstb = [Buf() for _ in range(NST)]
        wbf = [cx.sb(es, "wbf%d" % i, [128, 2, 512], BF16) for i in range(NBF)]
        wbfb = [Buf() for _ in range(NBF)]
        tmp = [cx.sb(es, "tmp%d" % i, [128, TT], F32) for i in range(2)]
        tmpb = [Buf() for _ in range(2)]
        mean = cx.sb(es, "mean", [128, TT], F32)
        rstd = cx.sb(es, "rstd", [128, TT], F32)
        stb = Buf()
        cnt = dict(w=0, t=0)
        if merge is not None:
            h2 = cx.sb(es, "h2", [128, DC, TT], BF16)
            h2b = Buf()
            gat = [cx.sb(es, "gat", [128, 4, TT], F32) for _ in range(2)]
            gatb = [Buf() for _ in range(2)]

        def load_w(src_ap, ncols):
            i = cnt["w"]
            cnt["w"] += 1
            st, sb_ = wst[i % NST], wstb[i % NST]
            bf, bb = wbf[i % NBF], wbfb[i % NBF]
            S.dma("sp", st[:, :, 0:ncols], src_ap.rearrange("(a p) n -> p a n", p=128), writes=[sb_])
            S.op("pool", lambda e: e.tensor_copy(out=bf[:, :, 0:ncols], in_=st[:, :, 0:ncols]), reads=[sb_], writes=[bb])
            return bf, bb

        for ti in range(NT):
            t0 = ti * TT
            S.dma("pool", z[:], xs_v[:, :, t0:t0 + TT], reads=[xsrc_bufs[ti]], writes=zb)
            if merge is not None:
                S.dma("pool", h[:], oa_v[:, :, t0:t0 + TT], reads=[merge["oa_bufs"][ti]], writes=hb)
                S.dma("pool", h2[:], ob_v[:, :, t0:t0 + TT], reads=[merge["ob_bufs"][ti]], writes=[h2b])
            elif hs_v is not None:
                S.dma("pool", h[:], hs_v[:, :, t0:t0 + TT], reads=[hsrc_bufs[ti]], writes=hb)
            for c in range(DC):
                if hs_v is None and merge is None:
                    S.op("dve", lambda e: e.tensor_scalar(out=h[:, c, :], in0=z[:, c, :],
                                                          scalar1=modT[:, sc_col + c:sc_col + c + 1],
                                                          scalar2=modT[:, sh_col + c:sh_col + c + 1],
                                                          op0=ALU.mult, op1=ALU.add),
                         reads=[zb[c], modb], writes=[hb[c]])
                S.op("act", lambda e: e.activation(out=z[:, c, :], in_=z[:, c, :], func=AF.Copy, scale=float(ALPHA)),
                     reads=[zb[c]], writes=[zb[c]])
            for pas in passes:
                fl = 0
                for (f0, nf) in pas:
                    gb = [cx.bank() for _ in range(nf)]
                    ub = [cx.bank() for _ in range(nf)]
                    for k2 in range(DC // 2):
                        for W, bk, hsrc, hbuf in ((Wg, gb, h, hb), (Wu, ub, (h2 if merge is not None else h),
                                                                    ([h2b] * DC if merge is not None else hb))):
                            wt, wtb = load_w(W[k2 * 256:(k2 + 1) * 256, f0 * 128:(f0 + nf) * 128], nf * 128)
                            for a in range(2):
                                kc = 2 * k2 + a
                                for j in range(nf):
                                    pt, pb = bk[j]
                                    S.op("pe", lambda e: e.matmul(pt[:], lhsT=wt[:, a, j * 128:(j + 1) * 128],
                                                                  rhs=hsrc[:, kc, :], start=(kc == 0), stop=(kc == DC - 1)),
                                         reads=[wtb, hbuf[kc]], writes=[pb])
                    if merge is not None:
                        pf = merge["proj_fm"]
                        for gi, (row0, bk) in enumerate(((FM_GA, gb), (FM_GB, ub))):
                            gt, gtb = gat[gi], gatb[gi]
                            S.dma("pool", gt[:], pf[row0 + f0 * 128:row0 + (f0 + 4) * 128, t0:t0 + TT].rearrange("(a p) t -> p a t", p=128),
                                  reads=[merge["fm_bufs"][ti]], writes=[gtb])
                            S.op("act", lambda e: e.activation(out=gt[:], in_=gt[:], func=AF.Sigmoid), reads=[gtb], writes=[gtb])
                            for j in range(4):
                                S.op("dve", lambda e: e.tensor_tensor(out=gt[:, j, :], in0=gt[:, j, :], in1=bk[j][0][:], op=ALU.mult),
                                     reads=[gtb, bk[j][1]], writes=[gtb])
                        for j in range(4):
                            S.op("dve", lambda e: e.tensor_tensor(out=a_sb[:, fl + j, :], in0=gat[0][:, j, :], in1=gat[1][:, j, :], op=ALU.add),
                                 reads=[gatb[0], gatb[1]], writes=[ab[fl + j]])
                    for j in (range(nf) if merge is None else ()):
                        tq, tqb = tmp[cnt["t"] % 2], tmpb[cnt["t"] % 2]
                        cnt["t"] += 1
                        S.op("act", lambda e: e.activation(out=tq[:], in_=gb[j][0][:], func=AF.Silu),
                             reads=[gb[j][1]], writes=[tqb])
                        S.op("dve", lambda e: e.tensor_tensor(out=a_sb[:, fl + j, :], in0=tq[:], in1=ub[j][0][:], op=ALU.mult),
                             reads=[tqb, ub[j][1]], writes=[ab[fl + j]])
                    fl += nf
                npf = fl
                fbase = pas[0][0]
                for dg in range(DC // 4):
                    bk = [cx.bank() for _ in range(4)]
                    for f2 in range(npf // 2):
                        wt, wtb = load_w(Wd[(fbase + 2 * f2) * 128:(fbase + 2 * f2 + 2) * 128, dg * 512:(dg + 1) * 512], 512)
                        for a in range(2):
                            fi = 2 * f2 + a
                            for j in range(4):
                                pt, pb = bk[j]
                                S.op("pe", lambda e: e.matmul(pt[:], lhsT=wt[:, a, j * 128:(j + 1) * 128],
                                                              rhs=a_sb[:, fi, :], start=(fi == 0), stop=(fi == npf - 1)),
                                     reads=[wtb, ab[fi]], writes=[pb])
                    for j in range(4):
                        c = dg * 4 + j
                        S.op("dve", lambda e: e.scalar_tensor_tensor(out=z[:, c, :], in0=bk[j][0][:], scalar=gs[:, c:c + 1],
                                                                     in1=z[:, c, :], op0=ALU.mult, op1=ALU.add),
                             reads=[bk[j][1], gsb, zb[c]], writes=[zb[c]])
            ps, psb = cx.bank()
            pq, pqb = cx.bank()
            for c in range(DC):
                tq, tqb = tmp[cnt["t"] % 2], tmpb[cnt["t"] % 2]
                cnt["t"] += 1
                S.op("act", lambda e: e.activation(out=tq[:], in_=z[:, c, :], func=AF.Square), reads=[zb[c]], writes=[tqb])
                S.op("pe", lambda e: e.matmul(ps[:], lhsT=ones[:], rhs=z[:, c, :], start=(c == 0), stop=(c == DC - 1)),
                     reads=[onesb, zb[c]], writes=[psb])
                S.op("pe", lambda e: e.matmul(pq[:], lhsT=ones[:], rhs=tq[:], start=(c == 0), stop=(c == DC - 1)),
                     reads=[onesb, tqb], writes=[pqb])
            S.op("act", lambda e: e.activation(out=mean[:], in_=ps[:], func=AF.Copy, scale=1.0 / D), reads=[psb], writes=[stb])
            S.op("dve", lambda e: e.tensor_tensor(out=rstd[:], in0=mean[:], in1=mean[:], op=ALU.mult), reads=[stb], writes=[stb])
            S.op("dve", lambda e: e.scalar_tensor_tensor(out=rstd[:], in0=pq[:], scalar=1.0 / D, in1=rstd[:],
                                                         op0=ALU.mult, op1=ALU.subtract), reads=[pqb, stb], writes=[stb])
            S.op("dve", lambda e: e.tensor_scalar(out=rstd[:], in0=rstd[:], scalar1=float(LN_EPS), scalar2=None, op0=ALU.add),
                 reads=[stb], writes=[stb])
            S.op("act", lambda e: e.activation(out=rstd[:], in_=rstd[:], func=AF.Sqrt), reads=[stb], writes=[stb])
            S.op("dve", lambda e: e.reciprocal(out=rstd[:], in_=rstd[:]), reads=[stb], writes=[stb])
            for c in range(DC):
                S.op("dve", lambda e: e.tensor_tensor(out=z[:, c, :], in0=z[:, c, :], in1=mean[:], op=ALU.subtract),
                     reads=[zb[c], stb], writes=[zb[c]])
                S.op("dve", lambda e: e.tensor_tensor(out=z[:, c, :], in0=z[:, c, :], in1=rstd[:], op=ALU.mult),
                     reads=[zb[c], stb], writes=[zb[c]])
                S.op("act", lambda e: e.activation(out=z[:, c, :], in_=z[:, c, :], func=AF.Identity,
                                                   scale=lng[:, c:c + 1], bias=lnb[:, c:c + 1]),
                     reads=[zb[c], lnbuf], writes=[zb[c]])
                if hd_v is not None:
                    S.op("dve", lambda e: e.tensor_scalar(out=h[:, c, :], in0=z[:, c, :],
                                                          scalar1=modT[:, nsc_col + c:nsc_col + c + 1],
                                                          scalar2=modT[:, nsh_col + c:nsh_col + c + 1],
                                                          op0=ALU.mult, op1=ALU.add),
                         reads=[zb[c], modb], writes=[hb[c]])
            S.dma("pool", xd_v[:, :, t0:t0 + TT], z[:], reads=zb, writes=[xdst_bufs[ti]])
            if hd_v is not None:
                S.dma("pool", hd_v[:, :, t0:t0 + TT], h[:], reads=hb, writes=[hdst_bufs[ti]])


def build(S_len, stages="all"):
    nc = bass.Bass("TRN2", target_bir_lowering=False)
    dt = lambda n, shp, t=F32, kind="ExternalInput": nc.dram_tensor(n, shp, t, kind=kind).ap()
    xT = dt("xT", [D, S_len])
    c_in = dt("c", [D])
    w_ada = dt("w_ada", [D, N_ADA * D])
    b_ada = dt("b_ada", [N_ADA * D])
    Wg1, Wu1, Wd1 = dt("ffn1_w_gate", [D, FF]), dt("ffn1_w_up", [D, FF]), dt("ffn1_w_down", [FF, D])
    Wg2, Wu2, Wd2 = dt("ffn2_w_gate", [D, FF]), dt("ffn2_w_up", [D, FF]), dt("ffn2_w_down", [FF, D])
    lnp = {k: dt(k, [D]) for k in ("ln1_g", "ln1_b", "ln2_g", "ln2_b", "ln3_g", "ln3_b")}
    w_in = dt("w_in", [D, IN_DIM])
    conv_w = dt("ssd_conv_w", [4, CONV_CH])
    conv_b = dt("ssd_conv_b", [CONV_CH])
    dt_bias = dt("ssd_dt_bias", [SSD_H])
    a_log = dt("ssd_a_log", [SSD_H])
    d_skip = dt("ssd_d", [SSD_H])
    norm_w = dt("ssd_norm_w", [D])
    Wa, Wb, Wo = dt("w_branch_a", [D, D]), dt("w_branch_b", [D, D]), dt("w_out", [D, D])
    ident_d = dt("ident", [128, 128])
    U_d = dt("Umat", [128, 128])
    NCP = S_len // 16
    positions = dt("positions", [S_len], I32)
    cmp_pos = dt("nsa_cmp_pos", [32, 128])
    kw1, kw2 = dt("nsa_cmp_k_w1", [D, 256]), dt("nsa_cmp_k_w2", [256, 128])
    vw1, vw2 = dt("nsa_cmp_v_w1", [D, 256]), dt("nsa_cmp_v_w2", [256, 128])
    C = dict(invf2=dt("c_invf2", [128, 32]), offs=dt("c_offs", [128, 32]), Esel=dt("c_esel", [128, S_len], BF16),
             winM=dt("c_winm", [128, 8, TT], BF16), cauM=dt("c_caum", [128, 4, TT], BF16),
             ovl1=dt("c_ovl1", [NCP, 129], BF16), maskC=dt("c_maskc", [NCP, S_len], BF16),
             addmask=dt("c_addmask", [S_len, 128]))
    qT = dt("qT", [D, S_len], BF16, "Internal")
    ksT = dt("ksT", [512, S_len], BF16, "Internal")
    kwT = dt("kwT", [512, S_len], BF16, "Internal")
    v_tm = dt("v_tm", [S_len, 1024], BF16, "Internal")
    KcT = dt("KcT", [512, NCP], BF16, "Internal")
    Vc_tm = dt("Vc_tm", [NCP, 512], BF16, "Internal")
    yT = dt("yT", [D, S_len], F32, "ExternalOutput")
    NT = S_len // TT
    x1T = dt("x1T", [D, S_len], F32, "Internal")
    h1T = dt("h1T", [D, S_len], BF16, "Internal")
    x2T = dt("x2T", [D, S_len], F32, "Internal")
    h2T = dt("h2T", [D, S_len], BF16, "Internal")
    nsp = max(1, S_len // 4096)
    proj_tm = RowSplit([(i * (S_len // nsp), S_len // nsp, dt("proj_tm%d" % i, [S_len // nsp, TM_W], F32, "Internal"))
                        for i in range(nsp)])
    proj_fm = RowSplit([(r0, n, dt("proj_fm%d" % r0, [n, S_len], F32, "Internal"))
                        for (r0, n) in ((0, FM_XBC), (FM_XBC, CONV_CH), (FM_GA, D), (FM_GB, D))])
    xbc_c = dt("xbc_c", [CONV_CH, S_len], F32, "Internal")
    oaT = dt("oaT", [D, S_len], BF16, "Internal")
    obT = dt("obT", [D, S_len], BF16, "Internal")
    with ExitStack() as es:
        cx = Ctx(nc, es)
        S = cx.S
        ident = cx.sb(es, "ident", [128, 128], F32)
        U = cx.sb(es, "U", [128, 128], F32)
        identb = Buf()
        S.dma("sp", ident[:], ident_d[:, :], writes=[identb])
        S.dma("sp", U[:], U_d[:, :], writes=[identb])
        ones = cx.sb(es, "ones", [128, 128], F32)
        onesb = Buf()
        S.op("dve", lambda e: e.memset(ones[:], 1.0), writes=[onesb])
        S.barrier()
        modT, modb = stage_ada(cx, es, c_in, w_ada, b_ada, ident)
        gsT = cx.sb(es, "gsT", [128, 3 * DC], F32)
        gsb = Buf()
        for s in (1, 4, 7):
            S.op("dve", lambda e: e.tensor_scalar(out=modT[:, s * DC:(s + 1) * DC], in0=modT[:, s * DC:(s + 1) * DC],
                                                  scalar1=1.0, scalar2=None, op0=ALU.add), reads=[modb], writes=[modb])
        for i, (s, f) in enumerate(((2, 0.5), (5, 1.0), (8, 0.5))):
            S.op("dve", lambda e: e.tensor_scalar(out=gsT[:, i * DC:(i + 1) * DC], in0=modT[:, s * DC:(s + 1) * DC],
                                                  scalar1=float(f), scalar2=None, op0=ALU.mult), reads=[modb], writes=[gsb])
        lnv = {}
        lnbuf = Buf()
        for k in lnp:
            t, b = load_vec_fm(cx, es, k, lnp[k], D, ident)
            lnv[k] = t
        S.barrier()
        mk = lambda: [Buf() for _ in range(NT)]
        xin_b, x1b, h1b, x2b, h2b, yb, tmb, fmb, oab, obb = (mk() for _ in range(10))
        only1 = (stages == "ffn1")
        ffn_stage(cx, "ffn1", S_len, xT, xin_b, None, None, modT, modb, 1 * DC, 0 * DC, Wg1, Wu1, Wd1,
                  gsT[:, 0:DC], gsb, lnv["ln1_g"], lnv["ln1_b"], lnbuf, 4 * DC, 3 * DC,
                  yT if only1 else x1T, yb if only1 else x1b, h1T, h1b, ones, onesb)
        if not only1:
            inproj_stage(cx, S_len, h1T, h1b, w_in, proj_tm, tmb, proj_fm, fmb)
            ssd_stage(cx, S_len, proj_tm, tmb, proj_fm, fmb, conv_w, conv_b, dt_bias, a_log, d_skip, norm_w,
                      xbc_c, obT, obb, ident, ones, U)
            if stages == "ssd":
                dbg_tm = dt("dbg_tm", [S_len, TM_W], F32, "ExternalOutput")
                dbg_xbc = dt("dbg_xbc", [CONV_CH, S_len], F32, "ExternalOutput")
                dbg_fm = dt("dbg_fm", [FM_H, S_len], F32, "ExternalOutput")
                S.dma("pool", dbg_xbc[:, 0:128], xbc_c[:, 0:128], reads=obb)
                with cx.scope() as tes:
                    tb16 = cx.sb(tes, "dbg16", [128, DC, TT], BF16)
                    tf32 = cx.sb(tes, "dbg32", [128, DC, TT], F32)
                    db = Buf()
                    for ti in range(NT):
                        S.dma("pool", tb16[:], obT.rearrange("(c p) t -> p c t", p=128)[:, :, ti * TT:(ti + 1) * TT], reads=[obb[ti]], writes=[db])
                        S.op("dve", lambda e: e.tensor_copy(out=tf32[:], in_=tb16[:]), reads=[db], writes=[db])
                        S.dma("pool", yT.rearrange("(c p) t -> p c t", p=128)[:, :, ti * TT:(ti + 1) * TT], tf32[:], reads=[db], writes=[yb[ti]])
            else:
                prep = nsa_prep_stage(cx, S_len, proj_tm, tmb, proj_fm, fmb, positions, cmp_pos, kw1, kw2, vw1, vw2,
                                      qT, ksT, kwT, v_tm, KcT, Vc_tm, ident, C)
                nsa_attn_stage(cx, S_len, prep, proj_fm, fmb, qT, ksT, kwT, v_tm, KcT, Vc_tm, oaT, oab, ident, ones, C)
                mg = dict(oaT=oaT, obT=obT, oa_bufs=oab, ob_bufs=obb, proj_fm=proj_fm, fm_bufs=fmb)
                ffn_stage(cx, "merge", S_len, x1T, x1b, None, None, modT, modb, 0, 0, Wa, Wb, Wo,
                          gsT[:, DC:2 * DC], gsb, lnv["ln2_g"], lnv["ln2_b"], lnbuf, 7 * DC, 6 * DC,
                          x2T, x2b, h2T, h2b, ones, onesb, merge=mg)
                ffn_stage(cx, "ffn2", S_len, x2T, x2b, h2T, h2b, modT, modb, 0, 0, Wg2, Wu2, Wd2,
                          gsT[:, 2 * DC:3 * DC], gsb, lnv["ln3_g"], lnv["ln3_b"], lnbuf, 0, 0,
                          yT, yb, None, None, ones, onesb)
        S.barrier()
    return nc


TWO_PI = 6.283185307179586
RC1 = 6.28125
RC2 = TWO_PI - RC1
PI = 3.141592653589793
ATT_SCALE = 128.0 ** -0.5
MASK_NEG = -30000.0


def rope_cs(cx, n, posi, pb, W):
    S = cx.S
    posf, ang, ki, kf, r, m, cs = W["posf"], W["ang"], W["ki"], W["kf"], W["r"], W["m"], W["cs"]
    wb = W["buf"]
    S.op("dve", lambda e: e.tensor_copy(out=posf[0:n, :], in_=posi[0:n, :]), reads=[pb], writes=[wb])
    S.op("dve", lambda e: e.scalar_tensor_tensor(out=ang[0:n, :], in0=W["invf2"][0:n, :], scalar=posf[0:n, 0:1],
                                                 in1=W["offs"][0:n, :], op0=ALU.mult, op1=ALU.add), reads=[wb], writes=[wb])
    S.op("dve", lambda e: e.tensor_scalar(out=ki[0:n, :], in0=ang[0:n, :], scalar1=1.0 / TWO_PI, scalar2=None, op0=ALU.mult),
         reads=[wb], writes=[wb])
    S.op("dve", lambda e: e.tensor_copy(out=kf[0:n, :], in_=ki[0:n, :]), reads=[wb], writes=[wb])
    S.op("dve", lambda e: e.scalar_tensor_tensor(out=r[0:n, :], in0=kf[0:n, :], scalar=-RC1, in1=ang[0:n, :],
                                                 op0=ALU.mult, op1=ALU.add), reads=[wb], writes=[wb])
    S.op("dve", lambda e: e.scalar_tensor_tensor(out=r[0:n, :], in0=kf[0:n, :], scalar=-RC2, in1=r[0:n, :],
                                                 op0=ALU.mult, op1=ALU.add), reads=[wb], writes=[wb])
    S.op("dve", lambda e: e.tensor_scalar(out=m[0:n, :], in0=r[0:n, :], scalar1=PI, scalar2=-TWO_PI, op0=ALU.is_gt, op1=ALU.mult),
         reads=[wb], writes=[wb])
    S.op("dve", lambda e: e.tensor_tensor(out=r[0:n, :], in0=r[0:n, :], in1=m[0:n, :], op=ALU.add), reads=[wb], writes=[wb])
    S.op("dve", lambda e: e.tensor_scalar(out=m[0:n, :], in0=r[0:n, :], scalar1=-PI, scalar2=TWO_PI, op0=ALU.is_lt, op1=ALU.mult),
         reads=[wb], writes=[wb])
    S.op("dve", lambda e: e.tensor_tensor(out=r[0:n, :], in0=r[0:n, :], in1=m[0:n, :], op=ALU.add), reads=[wb], writes=[wb])
    S.op("dve", lambda e: e.tensor_scalar(out=r[0:n, :], in0=r[0:n, :], scalar1=PI, scalar2=-PI, op0=ALU.min, op1=ALU.max),
         reads=[wb], writes=[wb])
    S.op("act", lambda e: e.activation(out=cs[0:n, :], in_=r[0:n, :], func=AF.Sin), reads=[wb], writes=[wb])
    return cs, wb


def rope_work(cx, es, invf2_d, offs_d):
    W = {}
    for k, shp, t in (("posf", [128, 1], F32), ("ang", [128, 32], F32), ("ki", [128, 32], I32), ("kf", [128, 32], F32),
                      ("r", [128, 32], F32), ("m", [128, 32], F32), ("cs", [128, 32], F32),
                      ("invf2", [128, 32], F32), ("offs", [128, 32], F32)):
        W[k] = cx.sb(es, "rw_" + k, shp, t)
    W["buf"] = Buf()
    cx.S.dma("sp", W["invf2"][:], invf2_d[:, :], writes=[W["buf"]])
    cx.S.dma("sp", W["offs"][:], offs_d[:, :], writes=[W["buf"]])
    for k in ("ta", "tb", "tc", "td"):
        W[k] = cx.sb(es, "rw_" + k, [128, 40, 16], F32)
    return W


def rope_apply(cx, n, X, H, xb, cs, csb, W):
    S = cx.S
    cosb = cs[0:n, 16:32].unsqueeze(1).to_broadcast([n, H, 16])
    sinb = cs[0:n, 0:16].unsqueeze(1).to_broadcast([n, H, 16])
    t1, t2 = X[:, :, 0:16], X[:, :, 16:32]
    ta, tb, tc, td = (W[k][0:n, 0:H, :] for k in ("ta", "tb", "tc", "td"))
    rb = W["buf"]
    S.op("dve", lambda e: e.tensor_tensor(out=ta, in0=t1, in1=cosb, op=ALU.mult), reads=[xb, csb], writes=[rb])
    S.op("dve", lambda e: e.tensor_tensor(out=tb, in0=t2, in1=sinb, op=ALU.mult), reads=[xb, csb], writes=[rb])
    S.op("dve", lambda e: e.tensor_tensor(out=tc, in0=t2, in1=cosb, op=ALU.mult), reads=[xb, csb], writes=[rb])
    S.op("dve", lambda e: e.tensor_tensor(out=td, in0=t1, in1=sinb, op=ALU.mult), reads=[xb, csb], writes=[rb])
    S.op("dve", lambda e: e.tensor_tensor(out=t1, in0=ta, in1=tb, op=ALU.subtract), reads=[rb], writes=[xb])
    S.op("dve", lambda e: e.tensor_tensor(out=t2, in0=tc, in1=td, op=ALU.add), reads=[rb], writes=[xb])


def nsa_prep_stage(cx, S_len, proj_tm, tm_bufs, proj_fm, fm_bufs, positions, cmp_pos, kw1, kw2, vw1, vw2,
                   qT, ksT, kwT, v_tm, KcT, Vc_tm, ident, C):
    S = cx.S
    done = Buf("nsa_prep")
    alltm, allfm = list(tm_bufs), list(fm_bufs)
    NCP = S_len // 16
    n_cmp = NCP - 1
    pos_v = positions.rearrange("(t o) -> t o", o=1)
    pos16 = positions.rearrange("(c r) -> c r", r=16)
    qT_v = qT.rearrange("(h d) t -> d h t", d=128)
    ksT_v = ksT.rearrange("(h d) t -> d h t", d=128)
    kwT_v = kwT.rearrange("(h d) t -> d h t", d=128)
    with cx.scope() as es:
        W = rope_work(cx, es, C["invf2"], C["offs"])
        qk = cx.sb(es, "qk", [128, 40 * 128], F32); qkb = Buf()
        qkT = cx.sb(es, "qkT", [128, 40, 128], BF16); qkTb = Buf()
        vv = cx.sb(es, "vv", [128, 1024], F32); vvb = Buf()
        vh = cx.sb(es, "vh", [128, 1024], BF16)
        posi = cx.sb(es, "posi", [128, 1], I32); pb = Buf()
        for st in range(S_len // 128):
            t0 = st * 128
            S.dma("pool", posi[:], pos_v[t0:t0 + 128, :], writes=[pb])
            cs, csb = rope_cs(cx, 128, posi, pb, W)
            S.dma("pool", qk[:], proj_tm[t0:t0 + 128, TM_Q:TM_Q + 5120], reads=alltm, writes=[qkb])
            S.dma("pool", vv[:], proj_tm[t0:t0 + 128, TM_VS:TM_VS + 1024], reads=alltm, writes=[vvb])
            rope_apply(cx, 128, qk[:].rearrange("p (h d) -> p h d", d=128), 40, qkb, cs, csb, W)
            for q4 in range(10):
                pt, ptb = cx.bank()
                for j in range(4):
                    hh = q4 * 4 + j
                    S.op("pe", lambda e: e.transpose(pt[:, j * 128:(j + 1) * 128], qk[:, hh * 128:(hh + 1) * 128], ident[:]),
                         reads=[qkb], writes=[ptb])
                S.op("act", lambda e: e.activation(out=qkT[:, q4 * 4:(q4 + 1) * 4, :],
                                                   in_=pt[:].rearrange("p (j n) -> p j n", j=4), func=AF.Copy),
                     reads=[ptb], writes=[qkTb])
            S.dma("pool", qT_v[:, :, t0:t0 + 128], qkT[:, 0:32, :], reads=[qkTb], writes=[done])
            S.dma("pool", ksT_v[:, :, t0:t0 + 128], qkT[:, 32:36, :], reads=[qkTb], writes=[done])
            S.dma("pool", kwT_v[:, :, t0:t0 + 128], qkT[:, 36:40, :], reads=[qkTb], writes=[done])
            S.op("pool", lambda e: e.tensor_copy(out=vh[:], in_=vv[:]), reads=[vvb], writes=[vvb])
            S.dma("pool", v_tm[t0:t0 + 128, :], vh[:], reads=[vvb], writes=[done])
    with cx.scope() as es:
        W = rope_work(cx, es, C["invf2"], C["offs"])
        posT, posTb = load_vec_fm(cx, es, "posT", cmp_pos.rearrange("l d -> (l d)"), 32 * 128, ident)
        posTh = cx.sb(es, "posTh", [128, 32], BF16)
        S.op("dve", lambda e: e.tensor_copy(out=posTh[:], in_=posT[:]), reads=[posTb], writes=[posTb])
        X = cx.sb(es, "X", [128, S_len + 16], F32); Xb = Buf()
        Xh = cx.sb(es, "Xh", [128, S_len + 16], BF16)
        w1s = cx.sb(es, "w1s", [128, 32, 256], F32); w1b = Buf()
        w1h = cx.sb(es, "w1h", [128, 32, 256], BF16)
        w2s = cx.sb(es, "w2s", [128, 2, 128], F32); w2b = Buf()
        w2h = cx.sb(es, "w2h", [128, 2, 128], BF16)
        sT = cx.sb(es, "sT", [128, 2, NCP], BF16); sTb = Buf()
        bia = cx.sb(es, "bia", [128, 2], F32); biab = Buf()
        kc = cx.sb(es, "kc", [128, 128], F32); kcb = Buf()
        kcT = cx.sb(es, "kcT", [128, 128], BF16); kcTb = Buf()
        vch = cx.sb(es, "vch", [128, 128], BF16); vchb = Buf()
        posi = cx.sb(es, "posi", [128, 1], I32); pb = Buf()
        X3 = Xh[:].rearrange("p (c r) -> p c r", r=16)
        for kind, (w1, w2, row0) in enumerate(((kw1, kw2, FM_KC), (vw1, vw2, FM_VC))):
            S.dma("sp", w1s[:], w1.rearrange("(l d) n -> d l n", d=128), writes=[w1b])
            S.op("pool", lambda e: e.tensor_copy(out=w1h[:], in_=w1s[:]), reads=[w1b], writes=[w1b])
            S.dma("sp", w2s[:], w2.rearrange("(a p) n -> p a n", p=128), writes=[w2b])
            S.op("pool", lambda e: e.tensor_copy(out=w2h[:], in_=w2s[:]), reads=[w2b], writes=[w2b])
            pbk, pbkb = cx.bank()
            for hc in range(2):
                for l in range(32):
                    S.op("pe", lambda e: e.matmul(pbk[:, hc:hc + 1], lhsT=w1h[:, l, hc * 128:(hc + 1) * 128], rhs=posTh[:, l:l + 1],
                                                  start=(l == 0), stop=(l == 31)), reads=[w1b, posTb], writes=[pbkb])
            S.op("dve", lambda e: e.tensor_copy(out=bia[:], in_=pbk[:, 0:2]), reads=[pbkb], writes=[biab])
            for g in range(NSA_G):
                S.op("dve", lambda e: e.memset(X[:, S_len:S_len + 16], 0.0), writes=[Xb])
                S.dma("pool", X[:, 0:S_len], proj_fm[row0 + g * 128:row0 + (g + 1) * 128, :], reads=allfm, writes=[Xb])
                S.op("pool", lambda e: e.tensor_copy(out=Xh[:], in_=X[:]), reads=[Xb], writes=[Xb])
                for hc in range(2):
                    for c0 in range(0, NCP, 512):
                        ncl = min(512, NCP - c0)
                        ph, phb = cx.bank()
                        for l in range(32):
                            rhs = X3[:, c0 + l // 16:c0 + l // 16 + ncl, l % 16]
                            S.op("pe", lambda e: e.matmul(ph[:, 0:ncl], lhsT=w1h[:, l, hc * 128:(hc + 1) * 128], rhs=rhs,
                                                          start=(l == 0), stop=(l == 31)), reads=[w1b, Xb], writes=[phb])
                        S.op("act", lambda e: e.activation(out=sT[:, hc, c0:c0 + ncl], in_=ph[:, 0:ncl], func=AF.Silu,
                                                           bias=bia[:, hc:hc + 1]), reads=[phb, biab], writes=[sTb])
                for ct in range(NCP // 128):
                    c0 = ct * 128
                    po, pob = cx.bank()
                    for hc in range(2):
                        S.op("pe", lambda e: e.matmul(po[:, 0:128], lhsT=sT[:, hc, c0:c0 + 128], rhs=w2h[:, hc, :],
                                                      start=(hc == 0), stop=(hc == 1)), reads=[sTb, w2b], writes=[pob])
                    if kind == 0:
                        n = min(128, n_cmp - c0)
                        S.op("dve", lambda e: e.memset(posi[:], 0), writes=[pb])
                        S.dma("pool", posi[0:n, :], pos16[c0 + 1:c0 + 1 + n, 15:16], writes=[pb], allow_slow_non_contiguous=True)
                        cs, csb = rope_cs(cx, 128, posi, pb, W)
                        S.op("dve", lambda e: e.tensor_copy(out=kc[:], in_=po[:, 0:128]), reads=[pob], writes=[kcb])
                        rope_apply(cx, 128, kc[:].rearrange("p (h d) -> p h d", d=128), 1, kcb, cs, csb, W)
                        pt, ptb = cx.bank()
                        S.op("pe", lambda e: e.transpose(pt[:, 0:128], kc[:], ident[:]), reads=[kcb], writes=[ptb])
                        S.op("act", lambda e: e.activation(out=kcT[:], in_=pt[:, 0:128], func=AF.Copy), reads=[ptb], writes=[kcTb])
                        S.dma("pool", KcT[g * 128:(g + 1) * 128, c0:c0 + 128], kcT[:], reads=[kcTb], writes=[done])
                    else:
                        S.op("act", lambda e: e.activation(out=vch[:], in_=po[:, 0:128], func=AF.Copy), reads=[pob], writes=[vchb])
                        S.dma("pool", Vc_tm[c0:c0 + 128, g * 128:(g + 1) * 128], vch[:], reads=[vchb], writes=[done])
    return done


def nsa_attn_stage(cx, S_len, prep, proj_fm, fm_bufs, qT, ksT, kwT, v_tm, KcT, Vc_tm, oaT, oa_bufs, ident, ones, C):
    S = cx.S
    NCP = S_len // 16
    NCT = NCP // 128
    NKT = S_len // 128
    NQG = S_len // TT
    allfm = list(fm_bufs)
    qT_v = qT.rearrange("(h d) t -> d h t", d=128)
    oa_v = oaT.rearrange("(h d) t -> d h t", d=128)
    with cx.scope() as es:
        cb = Buf("consts")
        identh = cx.sb(es, "identh", [128, 128], BF16)
        onesh = cx.sb(es, "onesh", [128, 128], BF16)
        S.op("dve", lambda e: e.tensor_copy(out=identh[:], in_=ident[:]), writes=[cb])
        S.op("dve", lambda e: e.memset(onesh[:], 1.0), writes=[cb])
        Esel = cx.sb(es, "Esel", [128, S_len], BF16)
        winM = cx.sb(es, "winM", [128, 8, TT], BF16)
        cauM = cx.sb(es, "cauM", [128, 4, TT], BF16)
        ovl = cx.sb(es, "ovl", [128, NCT, 129], BF16)
        S.dma("sp", Esel[:], C["Esel"][:, 0:S_len], writes=[cb])
        S.dma("sp", winM[:], C["winM"][:, :, :], writes=[cb])
        S.dma("sp", cauM[:], C["cauM"][:, :, :], writes=[cb])
        S.dma("sp", ovl[:], C["ovl1"].rearrange("(a p) n -> p a n", p=128), writes=[cb])
        S.barrier()
        KsT = cx.sb(es, "KsT", [128, S_len], BF16)
        KwT = cx.sb(es, "KwT", [128, S_len], BF16)
        Vs = cx.sb(es, "Vs", [128, NKT, 128], BF16)
        Vw = cx.sb(es, "Vw", [128, NKT, 128], BF16)
        Kc = cx.sb(es, "Kc", [128, NCP], BF16)
        Vc = cx.sb(es, "Vc", [128, NCT, 128], BF16)
        kvb = Buf("kv")
        Q = cx.sb(es, "Q", [128, 8, TT], BF16); Qb = Buf()
        gT = cx.sb(es, "gT", [96, TT], F32); gTb = Buf()
        gsel = cx.sb(es, "gsel", [96, TT], F32); gselb = Buf()
        mC = [cx.sb(es, "mC", [128, TT], BF16) for _ in range(2)]; mCb = Buf()
        amask = cx.sb(es, "amask", [128, 4, 128], F32); amb = Buf()
        acc = cx.sb(es, "acc", [128, 8, TT], F32); accb = [Buf() for _ in range(8)]
        acch = cx.sb(es, "acch", [128, 8, TT], BF16); acchb = Buf()
        PT = [cx.sb(es, "PT", [128, TT], BF16) for _ in range(3)]; PTb = [Buf() for _ in range(3)]
        PTc = cx.sb(es, "PTc", [128, NCT, TT], BF16); PTcb = Buf()
        impa = cx.sb(es, "impa", [128, 4, 128], F32); impb_ = Buf()
        rd = cx.sb(es, "rd", [128, 4], F32); rdb = Buf()
        sc = cx.sb(es, "sc", [128, 128], F32); scb = Buf()
        sc2 = cx.sb(es, "sc2", [128, 128], F32)
        m8 = cx.sb(es, "m8", [128, 8], F32)
        biasT = cx.sb(es, "biasT", [128, TT], BF16); biasb = Buf()
        rden = cx.sb(es, "rden", [128, TT], F32); rdenb = Buf()
        wgt = cx.sb(es, "wgt", [128, TT], F32); wgtb = Buf()
        tmpo = cx.sb(es, "tmpo", [128, TT], F32); tmpob = Buf()
        pi = [0]

        def combine(hl, r, Ob, Db, first):
            S.op("dve", lambda e: e.tensor_scalar(out=gsel[:], in0=gT[:], scalar1=ident[0:96, r:r + 1], scalar2=None, op0=ALU.mult),
                 reads=[gTb], writes=[gselb])
            pg, pgb = cx.bank()
            S.op("pe", lambda e: e.matmul(pg[:], lhsT=ones[0:96, :], rhs=gsel[:], start=True, stop=True), reads=[gselb], writes=[pgb])
            S.op("dve", lambda e: e.tensor_scalar(out=rden[:], in0=Db[0][:], scalar1=1e-30, scalar2=None, op0=ALU.max),
                 reads=[Db[1]], writes=[rdenb])
            S.op("dve", lambda e: e.reciprocal(out=rden[:], in_=rden[:]), reads=[rdenb], writes=[rdenb])
            S.op("dve", lambda e: e.tensor_tensor(out=wgt[:], in0=rden[:], in1=pg[:], op=ALU.mult), reads=[rdenb, pgb], writes=[wgtb])
            if first:
                S.op("dve", lambda e: e.tensor_tensor(out=acc[:, hl, :], in0=wgt[:], in1=Ob[0][:], op=ALU.mult),
                     reads=[wgtb, Ob[1]], writes=[accb[hl]])
            else:
                S.op("dve", lambda e: e.tensor_tensor(out=tmpo[:], in0=wgt[:], in1=Ob[0][:], op=ALU.mult),
                     reads=[wgtb, Ob[1]], writes=[tmpob])
                S.op("dve", lambda e: e.tensor_tensor(out=acc[:, hl, :], in0=acc[:, hl, :], in1=tmpo[:], op=ALU.add),
                     reads=[tmpob, accb[hl]], writes=[accb[hl]])

        def attend(hl, tiles, Ob, Db, keep=None):
            nt = len(tiles)
            for i, (kl, vl, extra) in enumerate(tiles):
                ps, psb = cx.bank()
                S.op("pe", lambda e: e.matmul(ps[:], lhsT=kl, rhs=Q[:, hl, :], start=True, stop=(len(extra) == 0)),
                     reads=[kvb, Qb], writes=[psb])
                for xi, (ml, mr, mb) in enumerate(extra):
                    S.op("pe", lambda e: e.matmul(ps[:], lhsT=ml, rhs=mr, start=False, stop=(xi == len(extra) - 1)),
                         reads=[cb] + mb, writes=[psb])
                if keep is None:
                    k = pi[0] % 3
                    pi[0] += 1
                    P, Pb = PT[k], PTb[k]
                    Pap = P[:]
                else:
                    Pap, Pb = keep[:, i, :], PTcb
                S.op("act", lambda e: e.activation(out=Pap, in_=ps[:], func=AF.Exp, scale=float(ATT_SCALE)), reads=[psb], writes=[Pb])
                S.op("pe", lambda e: e.matmul(Ob[0][:], lhsT=vl, rhs=Pap, start=(i == 0), stop=(i == nt - 1)),
                     reads=[kvb, Pb], writes=[Ob[1]])
                S.op("pe", lambda e: e.matmul(Db[0][:], lhsT=onesh[:], rhs=Pap, start=(i == 0), stop=(i == nt - 1)),
                     reads=[cb, Pb], writes=[Db[1]])

        for g in range(NSA_G):
            S.dma("pool", KsT[:], ksT[g * 128:(g + 1) * 128, :], reads=[prep], writes=[kvb])
            S.dma("pool", KwT[:], kwT[g * 128:(g + 1) * 128, :], reads=[prep], writes=[kvb])
            S.dma("pool", Vs[:], v_tm[:, g * 128:(g + 1) * 128].rearrange("(a p) d -> p a d", p=128), reads=[prep], writes=[kvb])
            S.dma("pool", Vw[:], v_tm[:, 512 + g * 128:512 + (g + 1) * 128].rearrange("(a p) d -> p a d", p=128), reads=[prep], writes=[kvb])
            S.dma("pool", Kc[:], KcT[g * 128:(g + 1) * 128, :], reads=[prep], writes=[kvb])
            S.dma("pool", Vc[:], Vc_tm[:, g * 128:(g + 1) * 128].rearrange("(a p) d -> p a d", p=128), reads=[prep], writes=[kvb])
            for qi in range(NQG):
                q0 = qi * TT
                S.dma("pool", Q[:], qT_v[:, g * 8:(g + 1) * 8, q0:q0 + TT], reads=[prep], writes=[Qb])
                S.dma("pool", gT[:], proj_fm[FM_GN:FM_GN + 96, q0:q0 + TT], reads=allfm, writes=[gTb])
                S.op("act", lambda e: e.activation(out=gT[:], in_=gT[:], func=AF.Sigmoid), reads=[gTb], writes=[gTb])
                S.dma("pool", amask[:], C["addmask"][q0:q0 + TT, :].rearrange("(a p) j -> p a j", p=128), writes=[amb])
                nct = min(NCT, ((q0 + 480) // 16) // 128 + 1)
                partial = [ct for ct in range(nct) if 16 * (ct * 128 + 127) + 31 > q0]
                assert len(partial) <= 2
                mct = {}
                for k, ct in enumerate(partial):
                    S.dma("pool", mC[k][:], C["maskC"][ct * 128:(ct + 1) * 128, q0:q0 + TT], writes=[mCb])
                    mct[ct] = mC[k]
                for hl in range(8):
                    Ob, Db = cx.bank(True), cx.bank(True)
                    tiles = []
                    for ct in range(nct):
                        extra = [(identh[:], mct[ct][:], [mCb])] if ct in mct else []
                        tiles.append((Kc[:, ct * 128:(ct + 1) * 128], Vc[:, ct, :], extra))
                    attend(hl, tiles, Ob, Db, keep=PTc)
                    ib = [cx.bank(True), cx.bank(True)]
                    for qs in range(4):
                        pt, ptb = ib[qs // 2]
                        co = (qs % 2) * 129
                        for ct in range(nct):
                            S.op("pe", lambda e: e.matmul(pt[:, co:co + 129], lhsT=PTc[:, ct, qs * 128:(qs + 1) * 128], rhs=ovl[:, ct, :],
                                                          start=(ct == 0), stop=(ct == nct - 1)), reads=[PTcb, cb], writes=[ptb])
                    for qs in range(4):
                        pt, ptb = ib[qs // 2]
                        co = (qs % 2) * 129
                        S.op("dve", lambda e: e.tensor_scalar(out=rd[:, qs:qs + 1], in0=pt[:, co + 128:co + 129], scalar1=1e-30, scalar2=None,
                                                              op0=ALU.max), reads=[ptb], writes=[rdb])
                        S.op("dve", lambda e: e.reciprocal(out=rd[:, qs:qs + 1], in_=rd[:, qs:qs + 1]), reads=[rdb], writes=[rdb])
                        if hl == 0:
                            S.op("dve", lambda e: e.tensor_scalar(out=impa[:, qs, :], in0=pt[:, co:co + 128], scalar1=rd[:, qs:qs + 1],
                                                                  scalar2=None, op0=ALU.mult), reads=[ptb, rdb], writes=[impb_])
                        else:
                            S.op("dve", lambda e: e.scalar_tensor_tensor(out=impa[:, qs, :], in0=pt[:, co:co + 128], scalar=rd[:, qs:qs + 1],
                                                                         in1=impa[:, qs, :], op0=ALU.mult, op1=ALU.add),
                                 reads=[ptb, rdb, impb_], writes=[impb_])
                    combine(hl, 0 * 32 + g * 8 + hl, Ob, Db, True)
                    cx.release(Ob, Db, ib[0], ib[1])
                pb_, pbb = cx.bank()
                for qs in range(4):
                    S.op("dve", lambda e: e.tensor_tensor(out=sc[:], in0=impa[:, qs, :], in1=amask[:, qs, :], op=ALU.add),
                         reads=[impb_, amb], writes=[scb])
                    S.op("dve", lambda e: e.max(out=m8[:], in_=sc[:]), reads=[scb], writes=[scb])
                    S.op("dve", lambda e: e.match_replace(out=sc2[:], in_to_replace=m8[:], in_values=sc[:], imm_value=-3.0e38),
                         reads=[scb], writes=[scb])
                    S.op("dve", lambda e: e.max(out=m8[:], in_=sc2[:]), reads=[scb], writes=[scb])
                    S.op("dve", lambda e: e.tensor_scalar(out=sc2[:], in0=sc[:], scalar1=m8[:, 7:8], scalar2=-1.0,
                                                          op0=ALU.is_ge, op1=ALU.add), reads=[scb], writes=[scb])
                    S.op("pe", lambda e: e.transpose(pb_[:, qs * 128:(qs + 1) * 128], sc2[:], ident[:]), reads=[scb], writes=[pbb])
                S.op("act", lambda e: e.activation(out=biasT[:], in_=pb_[:], func=AF.Copy, scale=float(-MASK_NEG)),
                     reads=[pbb], writes=[biasb])
                kt_hi = (q0 + TT - 1) // 128
                for hl in range(8):
                    Ob, Db = cx.bank(True), cx.bank(True)
                    tiles = []
                    for kt in range(kt_hi + 1):
                        extra = [(Esel[:, kt * 128:(kt + 1) * 128], biasT[:], [biasb])]
                        if kt * 128 >= q0:
                            extra.append((identh[:], cauM[:, kt - 4 * qi, :], []))
                        tiles.append((KsT[:, kt * 128:(kt + 1) * 128], Vs[:, kt, :], extra))
                    attend(hl, tiles, Ob, Db)
                    combine(hl, 1 * 32 + g * 8 + hl, Ob, Db, False)
                    cx.release(Ob, Db)
                    Ob, Db = cx.bank(True), cx.bank(True)
                    tiles = []
                    for r in range(8):
                        kt = 4 * qi - 4 + r
                        if kt < 0:
                            continue
                        tiles.append((KwT[:, kt * 128:(kt + 1) * 128], Vw[:, kt, :], [(identh[:], winM[:, r, :], [])]))
                    attend(hl, tiles, Ob, Db)
                    combine(hl, 2 * 32 + g * 8 + hl, Ob, Db, False)
                    cx.release(Ob, Db)
                S.op("act", lambda e: e.activation(out=acch[:], in_=acc[:], func=AF.Copy), reads=accb, writes=[acchb])
                S.dma("pool", oa_v[:, g * 8:(g + 1) * 8, q0:q0 + TT], acch[:], reads=[acchb], writes=[oa_bufs[qi]])


def nsa_stub_stage(cx, S_len, oaT, oa_bufs):
    S = cx.S
    with cx.scope() as es:
        zt = cx.sb(es, "zt", [128, DC, TT], BF16)
        zb = Buf()
        S.op("dve", lambda e: e.memset(zt[:], 0.0), writes=[zb])
        v = oaT.rearrange("(c p) t -> p c t", p=128)
        for ti in range(S_len // TT):
            S.dma("pool", v[:, :, ti * TT:(ti + 1) * TT], zt[:], reads=[zb], writes=[oa_bufs[ti]])


def nsa_consts(S_len):
    import ml_dtypes
    bf = ml_dtypes.bfloat16
    NCP = S_len // 16
    half = 16
    invf = (np.float32(500000.0) ** (-np.arange(half, dtype=np.float32) / np.float32(half))).astype(np.float32)
    invf2 = np.tile(np.concatenate([invf, invf])[None, :], (128, 1)).astype(np.float32)
    offs = np.tile(np.concatenate([np.zeros(16, np.float32), np.full(16, np.pi / 2, np.float32)])[None, :], (128, 1)).astype(np.float32)
    key = np.arange(S_len)
    esel = (key[None, :] // 64 == np.arange(128)[:, None]).astype(np.float32).astype(bf)
    p = np.arange(128)[:, None, None]
    r8 = np.arange(8)[None, :, None]
    q = np.arange(TT)[None, None, :]
    kp = r8 * 128 + p
    winm = np.where((kp - 512 <= q) & (kp > q), 0.0, MASK_NEG).astype(np.float32).astype(bf)
    r4 = np.arange(4)[None, :, None]
    caum = np.where(r4 * 128 + p <= q, 0.0, MASK_NEG).astype(np.float32).astype(bf)
    c = np.arange(NCP)
    j = np.arange(128)
    ovl = ((16 * c[:, None] < 64 * j[None, :] + 64) & (16 * c[:, None] + 31 >= 64 * j[None, :])).astype(np.float32)
    ovl[NCP - 1, :] = 0.0
    ovl1 = np.concatenate([ovl, np.ones((NCP, 1), np.float32)], axis=1).astype(bf)
    t = np.arange(S_len)
    maskc = np.where(16 * c[:, None] + 31 <= t[None, :], 0.0, MASK_NEG).astype(np.float32).astype(bf)
    cur = t // 64
    forced = (j[None, :] == 0) | (j[None, :] == cur[:, None]) | (j[None, :] == cur[:, None] - 1)
    causal = (64 * j[None, :] <= t[:, None])
    addmask = np.where(causal, np.where(forced, 1.0e4, 0.0), -1.0e30).astype(np.float32)
    return {"c_invf2": invf2, "c_offs": offs, "c_esel": np.ascontiguousarray(esel), "c_winm": np.ascontiguousarray(winm),
            "c_caum": np.ascontiguousarray(caum), "c_ovl1": np.ascontiguousarray(ovl1), "c_maskc": np.ascontiguousarray(maskc),
            "c_addmask": addmask}


INPUT_KEYS = ["nsa_cmp_pos", "nsa_cmp_k_w1", "nsa_cmp_k_w2", "nsa_cmp_v_w1", "nsa_cmp_v_w2", "c", "w_ada", "b_ada", "ffn1_w_gate", "ffn1_w_up", "ffn1_w_down", "w_in", "ssd_conv_w", "ssd_conv_b",
              "ssd_dt_bias", "ssd_a_log", "ssd_d", "ssd_norm_w", "w_branch_a", "w_branch_b", "w_out",
              "ffn2_w_gate", "ffn2_w_up", "ffn2_w_down", "ln1_g", "ln1_b", "ln2_g", "ln2_b", "ln3_g", "ln3_b"]


def make_in_maps(inputs, n_cores=8, batches=None):
    B = inputs["x"].shape[0]
    consts = {"ident": np.eye(128, dtype=np.float32), "Umat": np.triu(np.ones((128, 128), np.float32))}
    consts.update(nsa_consts(inputs["x"].shape[1]))
    shared = {}
    for k in INPUT_KEYS:
        a = np.asarray(inputs[k])
        if k != "c":
            a = a[0]
        shared[k] = np.ascontiguousarray(a, dtype=np.float32)
    maps = []
    for core in range(n_cores):
        b = (core * B) // n_cores if batches is None else batches[core]
        m = dict(shared)
        m["c"] = np.ascontiguousarray(shared["c"][b])
        m["xT"] = np.ascontiguousarray(np.asarray(inputs["x"])[b].T, dtype=np.float32)
        m["positions"] = np.ascontiguousarray(np.asarray(inputs["positions"])[b], dtype=np.int32)
        m.update(consts)
        maps.append(m)
    return maps


def kernel(**inputs):
    x = np.asarray(inputs["x"])
    B, S_len, _ = x.shape
    nc = build(S_len, "all")
    maps = make_in_maps(inputs)
    res = run_bass_kernel_spmd(nc, maps, core_ids=list(range(8)))
    out = np.empty((B, S_len, D), np.float32)
    for b in range(B):
        core = (b * 8) // B
        out[b] = res.results[core]["yT"].T
    return out


NSA_H, NSA_G, DH = 32, 4, 128
SSD_H, SSD_P, SSD_G, SSD_N = 64, 64, 8, 128
CONV_CH = D + 2 * SSD_G * SSD_N
C_Q, C_KC, C_VC, C_KS, C_VS, C_KW, C_VW, C_GN, C_Z, C_XBC, C_DT, C_GA, C_GB = (
    0, 4096, 4608, 5120, 5632, 6144, 6656, 7168, 7264, 11360, 17504, 17568, 21664)
IN_DIM = 25760
TM_SEGS = [(C_Q, 4096), (C_KS, 512), (C_KW, 512), (C_VS, 512), (C_VW, 512), (C_Z, 4096), (C_DT, 64)]
TM_Q, TM_KS, TM_KW, TM_VS, TM_VW, TM_Z, TM_DT = 0, 4096, 4608, 5120, 5632, 6144, 10240
TM_W = 10304
FM_SEGS = [(C_KC, 512), (C_VC, 512), (C_GN, 96), (C_XBC, CONV_CH), (C_GA, 4096), (C_GB, 4096)]
FM_KC, FM_VC, FM_GN, FM_XBC, FM_GA, FM_GB = 0, 512, 1024, 1152, 1152 + CONV_CH, 1152 + CONV_CH + 4096
FM_H = 1152 + CONV_CH + 8192


class RowSplit:
    def __init__(self, parts):
        self.parts = parts

    def __getitem__(self, idx):
        rs, cs = idx
        r0, r1 = rs.start, rs.stop
        for (p0, n, ap) in self.parts:
            if p0 <= r0 and r1 <= p0 + n:
                return ap[r0 - p0:r1 - p0, cs]
        raise IndexError((r0, r1))


class WLoader:
    def __init__(self, cx, es, nst=3, nbf=4):
        self.cx = cx
        self.wst = [cx.sb(es, "wst", [128, 2, 512], F32) for _ in range(nst)]
        self.wstb = [Buf() for _ in range(nst)]
        self.wbf = [cx.sb(es, "wbf", [128, 2, 512], BF16) for _ in range(nbf)]
        self.wbfb = [Buf() for _ in range(nbf)]
        self.i = 0

    def load(self, src_ap, ncols):
        S = self.cx.S
        i = self.i
        self.i += 1
        st, sb_ = self.wst[i % len(self.wst)], self.wstb[i % len(self.wst)]
        bf, bb = self.wbf[i % len(self.wbf)], self.wbfb[i % len(self.wbf)]
        S.dma("sp", st[:, :, 0:ncols], src_ap.rearrange("(a p) n -> p a n", p=128), writes=[sb_])
        S.op("pool", lambda e: e.tensor_copy(out=bf[:, :, 0:ncols], in_=st[:, :, 0:ncols]), reads=[sb_], writes=[bb])
        return bf, bb


def inproj_stage(cx, S_len, hT, h_bufs, w_in, proj_tm, tm_bufs, proj_fm, fm_bufs):
    S = cx.S
    NT = S_len // TT
    hs_v = hT.rearrange("(c p) t -> p c t", p=128)
    with cx.scope() as es:
        h = cx.sb(es, "h", [128, DC, TT], BF16)
        hb = Buf()
        wl = WLoader(cx, es)
        ost = [cx.sb(es, "ost", [128, 4, 512], F32) for _ in range(2)]
        ostb = [Buf() for _ in range(2)]
        oi = 0
        for ti in range(NT):
            t0 = ti * TT
            S.dma("pool", h[:], hs_v[:, :, t0:t0 + TT], reads=[h_bufs[ti]], writes=[hb])
            tcol = 0
            for (c0, n) in TM_SEGS:
                for g0 in range(0, n, 512):
                    nc_ = min(512, n - g0)
                    bk = [cx.bank() for _ in range(4)]
                    for k2 in range(DC // 2):
                        wt, wtb = wl.load(w_in[k2 * 256:(k2 + 1) * 256, c0 + g0:c0 + g0 + nc_], nc_)
                        for a in range(2):
                            kc = 2 * k2 + a
                            for ts in range(4):
                                S.op("pe", lambda e: e.matmul(bk[ts][0][:, 0:nc_], lhsT=h[:, kc, ts * 128:(ts + 1) * 128],
                                                              rhs=wt[:, a, 0:nc_], start=(kc == 0), stop=(kc == DC - 1)),
                                     reads=[wtb, hb], writes=[bk[ts][1]])
                    o, ob = ost[oi % 2], ostb[oi % 2]
                    oi += 1
                    for ts in range(4):
                        eng = "act" if ts % 2 == 0 else "dve"
                        if eng == "act":
                            S.op("act", lambda e: e.activation(out=o[:, ts, 0:nc_], in_=bk[ts][0][:, 0:nc_], func=AF.Copy),
                                 reads=[bk[ts][1]], writes=[ob])
                        else:
                            S.op("dve", lambda e: e.tensor_copy(out=o[:, ts, 0:nc_], in_=bk[ts][0][:, 0:nc_]),
                                 reads=[bk[ts][1]], writes=[ob])
                    dst = proj_tm[t0:t0 + TT, tcol + g0:tcol + g0 + nc_].rearrange("(a p) n -> p a n", p=128)
                    S.dma("pool", dst, o[:, :, 0:nc_], reads=[ob], writes=[tm_bufs[ti]])
                tcol += n
            frow = 0
            for (c0, n) in FM_SEGS:
                for g0 in range(0, n, 512):
                    nc_ = min(512, n - g0)
                    nf = (nc_ + 127) // 128
                    bk = [cx.bank() for _ in range(nf)]
                    for k2 in range(DC // 2):
                        wt, wtb = wl.load(w_in[k2 * 256:(k2 + 1) * 256, c0 + g0:c0 + g0 + nc_], nc_)
                        for a in range(2):
                            kc = 2 * k2 + a
                            for j in range(nf):
                                m = min(128, nc_ - j * 128)
                                S.op("pe", lambda e: e.matmul(bk[j][0][0:m, :], lhsT=wt[:, a, j * 128:j * 128 + m],
                                                              rhs=h[:, kc, :], start=(kc == 0), stop=(kc == DC - 1)),
                                     reads=[wtb, hb], writes=[bk[j][1]])
                    o, ob = ost[oi % 2], ostb[oi % 2]
                    oi += 1
                    for j in range(nf):
                        m = min(128, nc_ - j * 128)
                        if j % 2 == 0:
                            S.op("act", lambda e: e.activation(out=o[0:m, j, :], in_=bk[j][0][0:m, :], func=AF.Copy),
                                 reads=[bk[j][1]], writes=[ob])
                        else:
                            S.op("dve", lambda e: e.tensor_copy(out=o[0:m, j, :], in_=bk[j][0][0:m, :]),
                                 reads=[bk[j][1]], writes=[ob])
                    if nc_ % 128 == 0:
                        dst = proj_fm[frow + g0:frow + g0 + nc_, t0:t0 + TT].rearrange("(a p) t -> p a t", p=128)
                        S.dma("pool", dst, o[:, 0:nf, :], reads=[ob], writes=[fm_bufs[ti]])
                    else:
                        S.dma("pool", proj_fm[frow + g0:frow + g0 + nc_, t0:t0 + TT], o[0:nc_, 0, :], reads=[ob],
                              writes=[fm_bufs[ti]])
                frow += n if n != 96 else 128


RMS_EPS = 1e-5


def bcast_row(cx, es, name, vec_ap, n):
    t = cx.sb(es, name, [128, n], F32)
    b = Buf()
    cx.S.dma("sp", t[:], vec_ap.partition_broadcast(128), writes=[b])
    return t, b


def ssd_stage(cx, S_len, proj_tm, tm_bufs, proj_fm, fm_bufs, conv_w, conv_b, dt_bias, a_log, d_skip, norm_w,
              xbc_c, obT, ob_bufs, ident, ones, U):
    nc, S = cx.nc, cx.S
    NCC = CONV_CH // 128
    allfm = list(fm_bufs)
    alltm = list(tm_bufs)
    cbuf = Buf("xbc_c")
    with cx.scope() as es:
        wk = []
        for k in range(4):
            t, b = load_vec_fm(cx, es, "cw%d" % k, conv_w[k, :], CONV_CH, ident)
            wk.append(t)
        cbT, _ = load_vec_fm(cx, es, "cb", conv_b, CONV_CH, ident)
        cvb = Buf()
        S.barrier()
        xin = [cx.sb(es, "xin", [128, 3 + S_len], F32) for _ in range(2)]
        xinb = [Buf() for _ in range(2)]
        xo = [cx.sb(es, "xo", [128, S_len], F32) for _ in range(2)]
        xob = [Buf() for _ in range(2)]
        for cc in range(NCC):
            xi, xib = xin[cc % 2], xinb[cc % 2]
            o, ob = xo[cc % 2], xob[cc % 2]
            S.op("dve", lambda e: e.memset(xi[:, 0:3], 0.0), writes=[xib])
            S.dma("pool", xi[:, 3:3 + S_len], proj_fm[FM_XBC + cc * 128:FM_XBC + (cc + 1) * 128, :], reads=allfm, writes=[xib])
            for t0 in range(0, S_len, TT):
                S.op("dve", lambda e: e.tensor_scalar(out=o[:, t0:t0 + TT], in0=xi[:, 3 + t0:3 + t0 + TT],
                                                      scalar1=wk[3][:, cc:cc + 1], scalar2=cbT[:, cc:cc + 1],
                                                      op0=ALU.mult, op1=ALU.add), reads=[xib], writes=[ob])
                for k in range(3):
                    S.op("dve", lambda e: e.scalar_tensor_tensor(out=o[:, t0:t0 + TT], in0=xi[:, k + t0:k + t0 + TT],
                                                                 scalar=wk[k][:, cc:cc + 1], in1=o[:, t0:t0 + TT],
                                                                 op0=ALU.mult, op1=ALU.add), reads=[xib, ob], writes=[ob])
                S.op("act", lambda e: e.activation(out=o[:, t0:t0 + TT], in_=o[:, t0:t0 + TT], func=AF.Silu),
                     reads=[ob], writes=[ob])
            S.dma("pool", xbc_c[cc * 128:(cc + 1) * 128, :], o[:], reads=[ob], writes=[cbuf])
    NCH = S_len // 128
    xc_v = xbc_c[0:D, :].rearrange("(c p) t -> p c t", p=128)
    b_v = xbc_c[D:D + 1024, :].rearrange("(g p) t -> p g t", p=128)
    c_v = xbc_c[D + 1024:D + 2048, :].rearrange("(g p) t -> p g t", p=128)
    ob_v = obT.rearrange("(c p) t -> p c t", p=128)
    with cx.scope() as es:
        dtb, _ = bcast_row(cx, es, "dtb", dt_bias, SSD_H)
        Arep, _ = bcast_row(cx, es, "Arep", a_log, SSD_H)
        drep, _ = bcast_row(cx, es, "drep", d_skip, SSD_H)
        nwrep, _ = bcast_row(cx, es, "nwrep", norm_w, D)
        cst = Buf()
        S.barrier()
        S.op("act", lambda e: e.activation(out=Arep[:], in_=Arep[:], func=AF.Exp), writes=[cst])
        S.op("dve", lambda e: e.tensor_scalar(out=Arep[:], in0=Arep[:], scalar1=-1.0, scalar2=None, op0=ALU.mult),
             reads=[cst], writes=[cst])
        xT = cx.sb(es, "xT", [128, DC, 128], F32); xTb = Buf()
        BT = cx.sb(es, "BT", [128, 8, 128], F32); BTb = Buf()
        CT = cx.sb(es, "CT", [128, 8, 128], F32); CTb = Buf()
        BTh = cx.sb(es, "BTh", [128, 8, 128], BF16)
        CTh = cx.sb(es, "CTh", [128, 8, 128], BF16)
        Btm = cx.sb(es, "Btm", [128, 8, 128], BF16); Btmb = Buf()
        ztm = cx.sb(es, "ztm", [128, D], F32); zb = Buf()
        xtm = cx.sb(es, "xtm", [128, D], F32); xtb = Buf()
        xdt = cx.sb(es, "xdt", [128, D], BF16); xdb = Buf()
        ysb = cx.sb(es, "ysb", [128, D], F32); yb = Buf()
        state = cx.sb(es, "state", [128, D], F32); stb = Buf()
        stbf = cx.sb(es, "stbf", [128, D], BF16); sbb = Buf()
        obt = cx.sb(es, "obt", [128, DC, 128], BF16); obb = Buf()
        dt = cx.sb(es, "dt", [128, 64], F32); dtbuf = Buf()
        da = cx.sb(es, "da", [128, 64], F32)
        t64 = cx.sb(es, "t64", [128, 64], F32)
        cum = cx.sb(es, "cum", [128, 64], F32); cumb = Buf()
        dend = cx.sb(es, "dend", [128, 64], F32)
        etot = cx.sb(es, "etot", [128, 64], F32); eb = Buf()
        cumT = cx.sb(es, "cumT", [64, 128], F32); cTb = Buf()
        sm = cx.sb(es, "sm", [128, 128], F32); smb = Buf()
        xw = cx.sb(es, "xw", [128, 512], BF16); xwb = Buf()
        Gs = [cx.sb(es, "G", [128, 128], F32) for _ in range(2)]; Gb = [Buf() for _ in range(2)]
        Es = [cx.sb(es, "E", [128, 128], F32) for _ in range(2)]; Eb = [Buf() for _ in range(2)]
        MT = [cx.sb(es, "MT", [128, 128], BF16) for _ in range(2)]; MTb = [Buf() for _ in range(2)]
        Cs = [cx.sb(es, "Cs", [128, 128], BF16) for _ in range(2)]; Csb = [Buf() for _ in range(2)]
        t512 = cx.sb(es, "t512", [128, 512], F32); t5b = Buf()
        selT = [cx.sb(es, "selT", [64, 128], F32) for _ in range(2)]; selb = [Buf() for _ in range(2)]
        ssq = cx.sb(es, "ssq", [128, 8], F32); ssb = Buf()
        S.op("dve", lambda e: e.memset(state[:], 0.0), writes=[stb])
        S.op("dve", lambda e: e.memset(stbf[:], 0.0), writes=[sbb])
        hi = 0
        for ci in range(NCH):
            t0 = ci * 128
            S.dma("pool", xT[:], xc_v[:, :, t0:t0 + 128], reads=[cbuf], writes=[xTb])
            S.dma("pool", BT[:], b_v[:, :, t0:t0 + 128], reads=[cbuf], writes=[BTb])
            S.dma("pool", CT[:], c_v[:, :, t0:t0 + 128], reads=[cbuf], writes=[CTb])
            S.dma("pool", ztm[:], proj_tm[t0:t0 + 128, TM_Z:TM_Z + D], reads=alltm, writes=[zb])
            S.dma("pool", dt[:], proj_tm[t0:t0 + 128, TM_DT:TM_DT + 64], reads=alltm, writes=[dtbuf])
            S.op("dve", lambda e: e.tensor_tensor(out=dt[:], in0=dt[:], in1=dtb[:], op=ALU.add), reads=[dtbuf], writes=[dtbuf])
            S.op("act", lambda e: e.activation(out=t64[:], in_=dt[:], func=AF.Abs), reads=[dtbuf], writes=[dtbuf])
            S.op("act", lambda e: e.activation(out=t64[:], in_=t64[:], func=AF.Exp, scale=-1.0), reads=[dtbuf], writes=[dtbuf])
            S.op("act", lambda e: e.activation(out=t64[:], in_=t64[:], func=AF.Ln, bias=1.0), reads=[dtbuf], writes=[dtbuf])
            S.op("dve", lambda e: e.scalar_tensor_tensor(out=dt[:], in0=dt[:], scalar=0.0, in1=t64[:], op0=ALU.max, op1=ALU.add),
                 reads=[dtbuf], writes=[dtbuf])
            S.op("dve", lambda e: e.tensor_tensor(out=da[:], in0=dt[:], in1=Arep[:], op=ALU.mult), reads=[dtbuf, cst], writes=[dtbuf])
            p1, p1b = cx.bank()
            S.op("pe", lambda e: e.matmul(p1[:, 0:64], lhsT=U[:], rhs=da[:], start=True, stop=True), reads=[dtbuf], writes=[p1b])
            S.op("pe", lambda e: e.matmul(p1[:, 64:128], lhsT=ones[:], rhs=da[:], start=True, stop=True), reads=[dtbuf], writes=[p1b])
            S.op("dve", lambda e: e.tensor_copy(out=cum[:], in_=p1[:, 0:64]), reads=[p1b], writes=[cumb])
            S.op("dve", lambda e: e.tensor_tensor(out=dend[:], in0=p1[:, 64:128], in1=cum[:], op=ALU.subtract), reads=[p1b, cumb], writes=[eb])
            S.op("act", lambda e: e.activation(out=dend[:], in_=dend[:], func=AF.Exp), reads=[eb], writes=[eb])
            S.op("act", lambda e: e.activation(out=etot[:], in_=p1[:, 64:128], func=AF.Exp), reads=[p1b], writes=[eb])
            p2, p2b = cx.bank()
            S.op("pe", lambda e: e.transpose(p2[0:64, 0:128], cum[:], ident[:]), reads=[cumb], writes=[p2b])
            S.op("dve", lambda e: e.tensor_copy(out=cumT[:], in_=p2[0:64, 0:128]), reads=[p2b], writes=[cTb])
            S.op("pool", lambda e: e.tensor_copy(out=BTh[:], in_=BT[:]), reads=[BTb], writes=[BTb])
            S.op("pool", lambda e: e.tensor_copy(out=CTh[:], in_=CT[:]), reads=[CTb], writes=[CTb])
            for q4 in range(2):
                pt, ptb = cx.bank()
                for j in range(4):
                    g = q4 * 4 + j
                    S.op("pe", lambda e: e.transpose(pt[:, j * 128:(j + 1) * 128], BT[:, g, :], ident[:]), reads=[BTb], writes=[ptb])
                S.op("act", lambda e: e.activation(out=Btm[:, q4 * 4:(q4 + 1) * 4, :],
                                                   in_=pt[:].rearrange("p (j n) -> p j n", j=4), func=AF.Copy),
                     reads=[ptb], writes=[Btmb])
            for q4 in range(DC // 4):
                pt, ptb = cx.bank()
                for j in range(4):
                    c = q4 * 4 + j
                    S.op("pe", lambda e: e.transpose(pt[:, j * 128:(j + 1) * 128], xT[:, c, :], ident[:]), reads=[xTb], writes=[ptb])
                S.op("act", lambda e: e.activation(out=xtm[:, q4 * 512:(q4 + 1) * 512], in_=pt[:], func=AF.Copy),
                     reads=[ptb], writes=[xtb])
            S.op("dve", lambda e: e.tensor_tensor(out=xdt[:].rearrange("p (h q) -> p h q", q=64),
                                                  in0=xtm[:].rearrange("p (h q) -> p h q", q=64),
                                                  in1=dt[:].unsqueeze(2).to_broadcast([128, 64, 64]), op=ALU.mult),
                 reads=[xtb, dtbuf], writes=[xdb])
            for g in range(SSD_G):
                gs = slice(g * 512, (g + 1) * 512)
                ps_, psb = cx.bank()
                S.op("pe", lambda e: e.matmul(ps_[:, 0:128], lhsT=BTh[:, g, :], rhs=CTh[:, g, :], start=True, stop=True),
                     reads=[BTb, CTb], writes=[psb])
                S.op("dve", lambda e: e.tensor_tensor(out=sm[:], in0=ps_[:, 0:128], in1=U[:], op=ALU.mult), reads=[psb], writes=[smb])
                S.op("dve", lambda e: e.tensor_tensor(out=xw[:].rearrange("p (h q) -> p h q", q=64),
                                                      in0=xdt[:, gs].rearrange("p (h q) -> p h q", q=64),
                                                      in1=dend[:, g * 8:(g + 1) * 8].unsqueeze(2).to_broadcast([128, 8, 64]),
                                                      op=ALU.mult), reads=[xdb, eb], writes=[xwb])
                ykp = cx.bank(True)
                yk, ykb = ykp
                prs = [cx.bank(True), cx.bank(True)]
                for hl in range(8):
                    hh = g * 8 + hl
                    k = hi % 2
                    hi += 1
                    pr, prb = prs[k]
                    S.op("dve", lambda e: e.tensor_scalar(out=selT[k][:], in0=cumT[:], scalar1=ident[0:64, hh:hh + 1], scalar2=None,
                                                          op0=ALU.mult), reads=[cTb], writes=[selb[k]])
                    S.op("pe", lambda e: e.matmul(pr[:, 0:128], lhsT=ones[0:64, :], rhs=selT[k][:], start=True, stop=True),
                         reads=[selb[k]], writes=[prb])
                    S.op("dve", lambda e: e.tensor_scalar(out=Gs[k][:], in0=pr[:, 0:128], scalar1=cum[:, hh:hh + 1], scalar2=0.0,
                                                          op0=ALU.subtract, op1=ALU.min), reads=[prb, cumb], writes=[Gb[k]])
                    S.op("act", lambda e: e.activation(out=Gs[k][:], in_=Gs[k][:], func=AF.Exp), reads=[Gb[k]], writes=[Gb[k]])
                    S.op("dve", lambda e: e.tensor_tensor(out=MT[k][:], in0=Gs[k][:], in1=sm[:], op=ALU.mult),
                         reads=[Gb[k], smb], writes=[MTb[k]])
                    S.op("act", lambda e: e.activation(out=Es[k][:], in_=pr[:, 0:128], func=AF.Exp), reads=[prb], writes=[Eb[k]])
                    S.op("dve", lambda e: e.tensor_tensor(out=Cs[k][:], in0=Es[k][:], in1=CT[:, g, :], op=ALU.mult),
                         reads=[Eb[k], CTb], writes=[Csb[k]])
                    S.op("pe", lambda e: e.matmul(yk[:, hl * 64:(hl + 1) * 64], lhsT=MT[k][:], rhs=xdt[:, hh * 64:(hh + 1) * 64],
                                                  start=True, stop=False), reads=[MTb[k], xdb], writes=[ykb])
                    S.op("pe", lambda e: e.matmul(yk[:, hl * 64:(hl + 1) * 64], lhsT=Cs[k][:], rhs=stbf[:, hh * 64:(hh + 1) * 64],
                                                  start=False, stop=True), reads=[Csb[k], sbb], writes=[ykb])
                S.op("dve", lambda e: e.tensor_tensor(out=t512[:].rearrange("p (h q) -> p h q", q=64),
                                                      in0=xtm[:, gs].rearrange("p (h q) -> p h q", q=64),
                                                      in1=drep[:, g * 8:(g + 1) * 8].unsqueeze(2).to_broadcast([128, 8, 64]),
                                                      op=ALU.mult), reads=[xtb], writes=[t5b])
                S.op("dve", lambda e: e.tensor_tensor(out=ysb[:, gs], in0=t512[:], in1=yk[:], op=ALU.add),
                     reads=[t5b, ykb], writes=[yb])
                cx.release(ykp, prs[0], prs[1])
                pu, pub = cx.bank()
                S.op("pe", lambda e: e.matmul(pu[:], lhsT=Btm[:, g, :], rhs=xw[:], start=True, stop=True),
                     reads=[Btmb, xwb], writes=[pub])
                S.op("dve", lambda e: e.tensor_tensor(out=state[:, gs].rearrange("p (h q) -> p h q", q=64),
                                                      in0=state[:, gs].rearrange("p (h q) -> p h q", q=64),
                                                      in1=etot[:, g * 8:(g + 1) * 8].unsqueeze(2).to_broadcast([128, 8, 64]),
                                                      op=ALU.mult), reads=[stb, eb, sbb], writes=[stb])
                S.op("dve", lambda e: e.tensor_tensor(out=state[:, gs], in0=state[:, gs], in1=pu[:], op=ALU.add),
                     reads=[stb, pub], writes=[stb])
                S.op("act", lambda e: e.activation(out=stbf[:, gs], in_=state[:, gs], func=AF.Copy), reads=[stb], writes=[sbb])
            S.op("act", lambda e: e.activation(out=ztm[:], in_=ztm[:], func=AF.Silu), reads=[zb], writes=[zb])
            S.op("dve", lambda e: e.tensor_tensor(out=ysb[:], in0=ysb[:], in1=ztm[:], op=ALU.mult), reads=[yb, zb], writes=[yb])
            for g in range(SSD_G):
                gs = slice(g * 512, (g + 1) * 512)
                S.op("act", lambda e: e.activation(out=t512[:], in_=ysb[:, gs], func=AF.Square, accum_out=ssq[:, g:g + 1]),
                     reads=[yb], writes=[t5b, ssb])
            S.op("dve", lambda e: e.tensor_scalar(out=ssq[:], in0=ssq[:], scalar1=1.0 / 512, scalar2=float(RMS_EPS),
                                                  op0=ALU.mult, op1=ALU.add), reads=[ssb], writes=[ssb])
            S.op("act", lambda e: e.activation(out=ssq[:], in_=ssq[:], func=AF.Sqrt), reads=[ssb], writes=[ssb])
            S.op("dve", lambda e: e.reciprocal(out=ssq[:], in_=ssq[:]), reads=[ssb], writes=[ssb])
            S.op("dve", lambda e: e.tensor_tensor(out=ysb[:].rearrange("p (g q) -> p g q", q=512),
                                                  in0=ysb[:].rearrange("p (g q) -> p g q", q=512),
                                                  in1=ssq[:].unsqueeze(2).to_broadcast([128, 8, 512]), op=ALU.mult),
                 reads=[yb, ssb], writes=[yb])
            S.op("dve", lambda e: e.tensor_tensor(out=ysb[:], in0=ysb[:], in1=nwrep[:], op=ALU.mult), reads=[yb], writes=[yb])
            for q4 in range(DC // 4):
                pt, ptb = cx.bank()
                for j in range(4):
                    c = q4 * 4 + j
                    S.op("pe", lambda e: e.transpose(pt[:, j * 128:(j + 1) * 128], ysb[:, c * 128:(c + 1) * 128], ident[:]),
                         reads=[yb], writes=[ptb])
                S.op("act", lambda e: e.activation(out=obt[:, q4 * 4:(q4 + 1) * 4, :],
                                                   in_=pt[:].rearrange("p (j n) -> p j n", j=4), func=AF.Copy),
                     reads=[ptb], writes=[obb])
            S.dma("pool", ob_v[:, :, t0:t0 + 128], obt[:], reads=[obb], writes=[ob_bufs[ci // 4]])
```

```python
import numpy as np
from contextlib import ExitStack
import concourse.bass as bass
import concourse.mybir as mybir
from concourse.bass_utils import run_bass_kernel_spmd

F32 = mybir.dt.float32
BF16 = mybir.dt.bfloat16
I32 = mybir.dt.int32
AF = mybir.ActivationFunctionType
ALU = mybir.AluOpType

D = 4096
DC = D // 128
FF = 11008
FC = FF // 128
N_ADA = 9
LN_EPS = 1e-5
ALPHA = 2.0 ** 0.25
TT = 512


class Buf:
    __slots__ = ("w", "r", "name")

    def __init__(self, name=""):
        self.w = None
        self.r = {}
        self.name = name


class Sched:
    RING = 12

    def __init__(self, nc, es):
        self.nc = nc
        self.eng = dict(pe=nc.tensor, act=nc.scalar, dve=nc.vector, pool=nc.gpsimd, sp=nc.sync)
        self.sem = {k: es.enter_context(nc.semaphore("s_" + k)) for k in ("pe", "act", "dve", "pool")}
        self.cnt = dict.fromkeys(self.sem, 0)
        self.known = {k: {} for k in self.eng}
        self.rings = {q: [es.enter_context(nc.semaphore("d_%s%d" % (q, i))) for i in range(self.RING)]
                      for q in ("sp", "pool", "act")}
        self.ring_j = dict.fromkeys(self.rings, 0)
        self.ring_tok = {q: [None] * self.RING for q in self.rings}
        self.nwait = 0

    def _wait(self, e, tok):
        sem, val, key = tok
        if e == "pe" and key == "pe":
            return
        k = self.known[e]
        if k.get(key, 0) >= val:
            return
        self.eng[e].wait_ge(sem, val)
        self.nwait += 1
        k[key] = val

    def _deps(self, reads, writes):
        need = []
        for b in reads:
            if b.w is not None:
                need.append(b.w)
        for b in writes:
            if b.w is not None:
                need.append(b.w)
            need.extend(b.r.values())
        return need

    def _mark(self, tok, reads, writes):
        for b in reads:
            b.r[tok[2]] = tok
        for b in writes:
            b.w = tok
            b.r = {}

    def op(self, e, fn, reads=(), writes=()):
        for t in self._deps(reads, writes):
            self._wait(e, t)
        ins = fn(self.eng[e])
        self.cnt[e] += 1
        tok = (self.sem[e], self.cnt[e], e)
        ins.then_inc(self.sem[e], 1)
        self._mark(tok, reads, writes)
        return tok

    def dma(self, q, out, in_, reads=(), writes=(), **kw):
        need = self._deps(reads, writes)
        j = self.ring_j[q]
        slot = j % self.RING
        prev = self.ring_tok[q][slot]
        if prev is not None:
            need.append(prev)
        for t in need:
            self._wait(q, t)
        sem = self.rings[q][slot]
        val = 16 * (j // self.RING + 1)
        self.eng[q].dma_start(out=out, in_=in_, **kw).then_inc(sem, 16)
        tok = (sem, val, (q, slot))
        self.ring_j[q] = j + 1
        self.ring_tok[q][slot] = tok
        self._mark(tok, reads, writes)
        return tok

    def barrier(self):
        toks = [(self.sem[k], self.cnt[k], k) for k in self.sem if self.cnt[k] > 0]
        for q in self.rings:
            toks.extend(t for t in self.ring_tok[q] if t is not None)
        for e in self.eng:
            for t in toks:
                if t[2] == e and e != "pe":
                    pass
                sem, val, key = t
                k = self.known[e]
                if k.get(key, 0) >= val:
                    continue
                self.eng[e].wait_ge(sem, val)
                k[key] = val


class Ctx:
    def __init__(self, nc, es):
        self.nc = nc
        self.es = es
        self.S = Sched(nc, es)
        self.banks = []
        for i in range(8):
            t = es.enter_context(nc.psum_tensor("bank%d" % i, [128, 512], F32))
            self.banks.append((t, Buf("bank%d" % i)))
        self.bank_i = 0

    def bank(self, hold=False):
        held = getattr(self, "held", None)
        if held is None:
            held = self.held = set()
        for _ in range(8):
            i = self.bank_i % 8
            self.bank_i += 1
            if i not in held:
                if hold:
                    held.add(i)
                return self.banks[i]
        raise RuntimeError("all PSUM banks held")

    def release(self, *bs):
        for b in bs:
            for i, bb in enumerate(self.banks):
                if bb[1] is b[1]:
                    self.held.discard(i)

    def sb(self, es, name, shape, dt):
        self.uid = getattr(self, "uid", 0) + 1
        return es.enter_context(self.nc.sbuf_tensor("%s_%d" % (name, self.uid), shape, dt))

    def scope(self):
        return Scope(self)


class Scope:
    def __init__(self, cx):
        self.cx = cx
        self.es = ExitStack()

    def __enter__(self):
        self.es.__enter__()
        return self.es

    def __exit__(self, *a):
        self.cx.S.barrier()
        return self.es.__exit__(*a)


def load_vec_fm(cx, es, name, vec_ap, n, ident):
    nc, S = cx.nc, cx.S
    c = n // 128
    out = cx.sb(es, name, [128, c], F32)
    ob = Buf(name)
    v2 = vec_ap.rearrange("(c p) -> c p", p=128)
    with cx.scope() as tes:
        r0 = 0
        k = 0
        while r0 < c:
            rows = min(96, c - r0)
            tmp = cx.sb(tes, "%s_t%d" % (name, k), [rows, 128], F32)
            tb = Buf()
            S.dma("sp", tmp[:], v2[r0:r0 + rows, :], writes=[tb])
            pt, pb = cx.bank()
            S.op("pe", lambda e: e.transpose(pt[:, 0:rows], tmp[:], ident[0:rows, 0:rows]), reads=[tb], writes=[pb])
            S.op("dve", lambda e: e.tensor_copy(out[:, r0:r0 + rows], pt[:, 0:rows]), reads=[pb], writes=[ob])
            r0 += rows
            k += 1
    return out, ob


def stage_ada(cx, es, c_ap, w_ada, b_ada, ident):
    nc, S = cx.nc, cx.S
    NCOL = N_ADA * DC
    modT = cx.sb(es, "modT", [128, NCOL], F32)
    modb = Buf("modT")
    with cx.scope() as tes:
        cT, cb = load_vec_fm(cx, tes, "cT", c_ap, D, ident)
        bT, bb = load_vec_fm(cx, tes, "bT", b_ada, N_ADA * D, ident)
        S.op("act", lambda e: e.activation(out=cT[:], in_=cT[:], func=AF.Silu), reads=[cb], writes=[cb])
        NB = 4
        wbuf = [cx.sb(tes, "wada%d" % i, [128, 4, 512], F32) for i in range(NB)]
        wb = [Buf() for _ in range(NB)]
        wi = 0
        pbanks = [cx.bank() for _ in range(4)]
        for ng in range(N_ADA * D // 512):
            tiles = []
            for k4 in range(DC // 4):
                t, b = wbuf[wi % NB], wb[wi % NB]
                wi += 1
                src = w_ada[k4 * 512:(k4 + 1) * 512, ng * 512:(ng + 1) * 512].rearrange("(a p) n -> p a n", p=128)
                S.dma("sp", t[:], src, writes=[b])
                for a in range(4):
                    kc = k4 * 4 + a
                    for j in range(4):
                        pt, pb = pbanks[j]
                        S.op("pe", lambda e: e.matmul(pt[:, ng:ng + 1], lhsT=t[:, a, j * 128:(j + 1) * 128],
                                                      rhs=cT[:, kc:kc + 1], start=(kc == 0), stop=(kc == DC - 1)),
                             reads=[b, cb], writes=[pb])
        mv = modT[:].rearrange("p (g j) -> p g j", j=4)
        bv = bT[:].rearrange("p (g j) -> p g j", j=4)
        for j in range(4):
            pt, pb = pbanks[j]
            S.op("dve", lambda e: e.tensor_tensor(out=mv[:, :, j], in0=pt[:, 0:NCOL // 4], in1=bv[:, :, j], op=ALU.add),
                 reads=[pb, bb], writes=[modb])
    return modT, modb


def ffn_stage(cx, name, S_len, xT_src, xsrc_bufs, hT_src, hsrc_bufs, modT, modb, sc_col, sh_col, Wg, Wu, Wd,
              gs, gsb, lng, lnb, lnbuf, nsc_col, nsh_col, xT_dst, xdst_bufs, hT_dst, hdst_bufs, ones, onesb, merge=None):
    nc, S = cx.nc, cx.S
    NT = S_len // TT
    xs_v = xT_src.rearrange("(c p) t -> p c t", p=128)
    xd_v = xT_dst.rearrange("(c p) t -> p c t", p=128)
    hs_v = hT_src.rearrange("(c p) t -> p c t", p=128) if hT_src is not None else None
    hd_v = hT_dst.rearrange("(c p) t -> p c t", p=128) if hT_dst is not None else None
    fgroups = [(f0, min(4, FC - f0)) for f0 in range(0, FC, 4)]
    passes = [fgroups[i:i + 6] for i in range(0, len(fgroups), 6)]
    APASS = 24
    if merge is not None:
        passes = [[(f0, 4) for f0 in range(0, DC, 4)]]
        APASS = DC
        oa_v = merge["oaT"].rearrange("(c p) t -> p c t", p=128)
        ob_v = merge["obT"].rearrange("(c p) t -> p c t", p=128)
    with cx.scope() as es:
        z = cx.sb(es, "z", [128, DC, TT], F32)
        zb = [Buf() for _ in range(DC)]
        h = cx.sb(es, "h", [128, DC, TT], BF16)
        hb = [Buf() for _ in range(DC)]
        a_sb = cx.sb(es, "a", [128, APASS, TT], BF16)
        ab = [Buf() for _ in range(APASS)]
        NST, NBF = (1, 4) if merge is not None else (1, 8)
        wbf = [cx.sb(es, "wbf%d" % i, [128, 2, 512], BF16) for i in range(NBF)]
        wbfb = [Buf() for _ in range(NBF)]
        tmp = [cx.sb(es, "tmp%d" % i, [128, TT], F32) for i in range(2)]
        tmpb = [Buf() for _ in range(2)]
        mean = cx.sb(es, "mean", [128, TT], F32)
        rstd = cx.sb(es, "rstd", [128, TT], F32)
        stb = Buf()
        cnt = dict(w=0, t=0)
        if merge is not None:
            h2 = cx.sb(es, "h2", [128, DC, TT], BF16)
            h2b = Buf()
            gat = [cx.sb(es, "gat", [128, 4, TT], F32) for _ in range(2)]
            gatb = [Buf() for _ in range(2)]

        def load_w(src_ap, ncols):
            i = cnt["w"]
            cnt["w"] += 1
            bf, bb = wbf[i % NBF], wbfb[i % NBF]
            S.dma("sp", bf[:, :, 0:ncols], src_ap.rearrange("(a p) n -> p a n", p=128),
                  reads=[cx.wbufs[src_ap.tensor.name]], writes=[bb])
            return bf, bb

        for ti in range(NT):
            t0 = ti * TT
            S.dma("pool", z[:], xs_v[:, :, t0:t0 + TT], reads=[xsrc_bufs[ti]], writes=zb)
            if merge is not None:
                S.dma("pool", h[:], oa_v[:, :, t0:t0 + TT], reads=[merge["oa_bufs"][ti]], writes=hb)
                S.dma("pool", h2[:], ob_v[:, :, t0:t0 + TT], reads=[merge["ob_bufs"][ti]], writes=[h2b])
            elif hs_v is not None:
                S.dma("pool", h[:], hs_v[:, :, t0:t0 + TT], reads=[hsrc_bufs[ti]], writes=hb)
            for c in range(DC):
                if hs_v is None and merge is None:
                    S.op("dve", lambda e: e.tensor_scalar(out=h[:, c, :], in0=z[:, c, :],
                                                          scalar1=modT[:, sc_col + c:sc_col + c + 1],
                                                          scalar2=modT[:, sh_col + c:sh_col + c + 1],
                                                          op0=ALU.mult, op1=ALU.add),
                         reads=[zb[c], modb], writes=[hb[c]])
                S.op("act", lambda e: e.activation(out=z[:, c, :], in_=z[:, c, :], func=AF.Copy, scale=float(ALPHA)),
                     reads=[zb[c]], writes=[zb[c]])
            for pas in passes:
                fl = 0
                for (f0, nf) in pas:
                    gb = [cx.bank() for _ in range(nf)]
                    ub = [cx.bank() for _ in range(nf)]
                    for k2 in range(DC // 2):
                        for W, bk, hsrc, hbuf in ((Wg, gb, h, hb), (Wu, ub, (h2 if merge is not None else h),
                                                                    ([h2b] * DC if merge is not None else hb))):
                            wt, wtb = load_w(W[k2 * 256:(k2 + 1) * 256, f0 * 128:(f0 + nf) * 128], nf * 128)
                            for a in range(2):
                                kc = 2 * k2 + a
                                for j in range(nf):
                                    pt, pb = bk[j]
                                    S.op("pe", lambda e: e.matmul(pt[:], lhsT=wt[:, a, j * 128:(j + 1) * 128],
                                                                  rhs=hsrc[:, kc, :], start=(kc == 0), stop=(kc == DC - 1)),
                                         reads=[wtb, hbuf[kc]], writes=[pb])
                    if merge is not None:
                        pf = merge["proj_fm"]
                        for gi, (row0, bk) in enumerate(((FM_GA, gb), (FM_GB, ub))):
                            gt, gtb = gat[gi], gatb[gi]
                            S.dma("pool", gt[:], pf[row0 + f0 * 128:row0 + (f0 + 4) * 128, t0:t0 + TT].rearrange("(a p) t -> p a t", p=128),
                                  reads=[merge["fm_bufs"][ti]], writes=[gtb])
                            S.op("act", lambda e: e.activation(out=gt[:], in_=gt[:], func=AF.Sigmoid), reads=[gtb], writes=[gtb])
                            for j in range(4):
                                S.op("dve", lambda e: e.tensor_tensor(out=gt[:, j, :], in0=gt[:, j, :], in1=bk[j][0][:], op=ALU.mult),
                                     reads=[gtb, bk[j][1]], writes=[gtb])
                        for j in range(4):
                            S.op("dve", lambda e: e.tensor_tensor(out=a_sb[:, fl + j, :], in0=gat[0][:, j, :], in1=gat[1][:, j, :], op=ALU.add),
                                 reads=[gatb[0], gatb[1]], writes=[ab[fl + j]])
                    for j in (range(nf) if merge is None else ()):
                        tq, tqb = tmp[cnt["t"] % 2], tmpb[cnt["t"] % 2]
                        cnt["t"] += 1
                        S.op("act", lambda e: e.activation(out=tq[:], in_=gb[j][0][:], func=AF.Silu),
                             reads=[gb[j][1]], writes=[tqb])
                        S.op("dve", lambda e: e.tensor_tensor(out=a_sb[:, fl + j, :], in0=tq[:], in1=ub[j][0][:], op=ALU.mult),
                             reads=[tqb, ub[j][1]], writes=[ab[fl + j]])
                    fl += nf
                npf = fl
                fbase = pas[0][0]
                for dg in range(DC // 4):
                    bk = [cx.bank() for _ in range(4)]
                    for f2 in range(npf // 2):
                        wt, wtb = load_w(Wd[(fbase + 2 * f2) * 128:(fbase + 2 * f2 + 2) * 128, dg * 512:(dg + 1) * 512], 512)
                        for a in range(2):
                            fi = 2 * f2 + a
                            for j in range(4):
                                pt, pb = bk[j]
                                S.op("pe", lambda e: e.matmul(pt[:], lhsT=wt[:, a, j * 128:(j + 1) * 128],
                                                              rhs=a_sb[:, fi, :], start=(fi == 0), stop=(fi == npf - 1)),
                                     reads=[wtb, ab[fi]], writes=[pb])
                    for j in range(4):
                        c = dg * 4 + j
                        S.op("dve", lambda e: e.scalar_tensor_tensor(out=z[:, c, :], in0=bk[j][0][:], scalar=gs[:, c:c + 1],
                                                                     in1=z[:, c, :], op0=ALU.mult, op1=ALU.add),
                             reads=[bk[j][1], gsb, zb[c]], writes=[zb[c]])
            ps, psb = cx.bank()
            pq, pqb = cx.bank()
            for c in range(DC):
                tq, tqb = tmp[cnt["t"] % 2], tmpb[cnt["t"] % 2]
                cnt["t"] += 1
                S.op("act", lambda e: e.activation(out=tq[:], in_=z[:, c, :], func=AF.Square), reads=[zb[c]], writes=[tqb])
                S.op("pe", lambda e: e.matmul(ps[:], lhsT=ones[:], rhs=z[:, c, :], start=(c == 0), stop=(c == DC - 1)),
                     reads=[onesb, zb[c]], writes=[psb])
                S.op("pe", lambda e: e.matmul(pq[:], lhsT=ones[:], rhs=tq[:], start=(c == 0), stop=(c == DC - 1)),
                     reads=[onesb, tqb], writes=[pqb])
            S.op("act", lambda e: e.activation(out=mean[:], in_=ps[:], func=AF.Copy, scale=1.0 / D), reads=[psb], writes=[stb])
            S.op("dve", lambda e: e.tensor_tensor(out=rstd[:], in0=mean[:], in1=mean[:], op=ALU.mult), reads=[stb], writes=[stb])
            S.op("dve", lambda e: e.scalar_tensor_tensor(out=rstd[:], in0=pq[:], scalar=1.0 / D, in1=rstd[:],
                                                         op0=ALU.mult, op1=ALU.subtract), reads=[pqb, stb], writes=[stb])
            S.op("dve", lambda e: e.tensor_scalar(out=rstd[:], in0=rstd[:], scalar1=float(LN_EPS), scalar2=None, op0=ALU.add),
                 reads=[stb], writes=[stb])
            S.op("act", lambda e: e.activation(out=rstd[:], in_=rstd[:], func=AF.Sqrt), reads=[stb], writes=[stb])
            S.op("dve", lambda e: e.reciprocal(out=rstd[:], in_=rstd[:]), reads=[stb], writes=[stb])
            for c in range(DC):
                S.op("dve", lambda e: e.tensor_tensor(out=z[:, c, :], in0=z[:, c, :], in1=mean[:], op=ALU.subtract),
                     reads=[zb[c], stb], writes=[zb[c]])
                S.op("dve", lambda e: e.tensor_tensor(out=z[:, c, :], in0=z[:, c, :], in1=rstd[:], op=ALU.mult),
                     reads=[zb[c], stb], writes=[zb[c]])
                S.op("act", lambda e: e.activation(out=z[:, c, :], in_=z[:, c, :], func=AF.Identity,
                                                   scale=lng[:, c:c + 1], bias=lnb[:, c:c + 1]),
                     reads=[zb[c], lnbuf], writes=[zb[c]])
                if hd_v is not None:
                    S.op("dve", lambda e: e.tensor_scalar(out=h[:, c, :], in0=z[:, c, :],
                                                          scalar1=modT[:, nsc_col + c:nsc_col + c + 1],
                                                          scalar2=modT[:, nsh_col + c:nsh_col + c + 1],
                                                          op0=ALU.mult, op1=ALU.add),
                         reads=[zb[c], modb], writes=[hb[c]])
            S.dma("pool", xd_v[:, :, t0:t0 + TT], z[:], reads=zb, writes=[xdst_bufs[ti]])
            if hd_v is not None:
                S.dma("pool", hd_v[:, :, t0:t0 + TT], h[:], reads=hb, writes=[hdst_bufs[ti]])


def build(S_len, stages="all"):
    nc = bass.Bass("TRN2", target_bir_lowering=False)
    dt = lambda n, shp, t=F32, kind="ExternalInput": nc.dram_tensor(n, shp, t, kind=kind).ap()
    xT = dt("xT", [D, S_len])
    c_in = dt("c", [D])
    w_ada = dt("w_ada", [D, N_ADA * D])
    b_ada = dt("b_ada", [N_ADA * D])
    Wg1, Wu1, Wd1 = dt("ffn1_w_gate", [D, FF]), dt("ffn1_w_up", [D, FF]), dt("ffn1_w_down", [FF, D])
    Wg2, Wu2, Wd2 = dt("ffn2_w_gate", [D, FF]), dt("ffn2_w_up", [D, FF]), dt("ffn2_w_down", [FF, D])
    lnp = {k: dt(k, [D]) for k in ("ln1_g", "ln1_b", "ln2_g", "ln2_b", "ln3_g", "ln3_b")}
    w_in = dt("w_in", [D, IN_DIM])
    conv_w = dt("ssd_conv_w", [4, CONV_CH])
    conv_b = dt("ssd_conv_b", [CONV_CH])
    dt_bias = dt("ssd_dt_bias", [SSD_H])
    a_log = dt("ssd_a_log", [SSD_H])
    d_skip = dt("ssd_d", [SSD_H])
    norm_w = dt("ssd_norm_w", [D])
    Wa, Wb, Wo = dt("w_branch_a", [D, D]), dt("w_branch_b", [D, D]), dt("w_out", [D, D])
    ident_d = dt("ident", [128, 128])
    U_d = dt("Umat", [128, 128])
    NCP = S_len // 16
    positions = dt("positions", [S_len], I32)
    cmp_pos = dt("nsa_cmp_pos", [32, 128])
    kw1, kw2 = dt("nsa_cmp_k_w1", [D, 256]), dt("nsa_cmp_k_w2", [256, 128])
    vw1, vw2 = dt("nsa_cmp_v_w1", [D, 256]), dt("nsa_cmp_v_w2", [256, 128])
    C = dict(invf2=dt("c_invf2", [128, 32]), offs=dt("c_offs", [128, 32]), Esel=dt("c_esel", [128, S_len], BF16),
             winM=dt("c_winm", [128, 8, TT], BF16), cauM=dt("c_caum", [128, 4, TT], BF16),
             ovl1=dt("c_ovl1", [NCP, 129], BF16), maskC=dt("c_maskc", [NCP, S_len], BF16),
             addmask=dt("c_addmask", [S_len, 128]))
    qT = dt("qT", [D, S_len], BF16, "Internal")
    ksT = dt("ksT", [512, S_len], BF16, "Internal")
    kwT = dt("kwT", [512, S_len], BF16, "Internal")
    v_tm = dt("v_tm", [S_len, 1024], BF16, "Internal")
    KcT = dt("KcT", [512, NCP], BF16, "Internal")
    Vc_tm = dt("Vc_tm", [NCP, 512], BF16, "Internal")
    yT = dt("yT", [D, S_len], F32, "ExternalOutput")
    NT = S_len // TT
    x1T = dt("x1T", [D, S_len], F32, "Internal")
    h1T = dt("h1T", [D, S_len], BF16, "Internal")
    x2T = dt("x2T", [D, S_len], F32, "Internal")
    h2T = dt("h2T", [D, S_len], BF16, "Internal")
    nsp = max(1, S_len // 4096)
    proj_tm = RowSplit([(i * (S_len // nsp), S_len // nsp, dt("proj_tm%d" % i, [S_len // nsp, TM_W], F32, "Internal"))
                        for i in range(nsp)])
    proj_fm = RowSplit([(r0, n, dt("proj_fm%d" % r0, [n, S_len], F32, "Internal"))
                        for (r0, n) in ((0, FM_XBC), (FM_XBC, CONV_CH), (FM_GA, D), (FM_GB, D))])
    xbc_c = dt("xbc_c", [CONV_CH, S_len], F32, "Internal")
    oaT = dt("oaT", [D, S_len], BF16, "Internal")
    obT = dt("obT", [D, S_len], BF16, "Internal")
    with ExitStack() as es:
        cx = Ctx(nc, es)
        S = cx.S
        ident = cx.sb(es, "ident", [128, 128], F32)
        U = cx.sb(es, "U", [128, 128], F32)
        identb = Buf()
        S.dma("sp", ident[:], ident_d[:, :], writes=[identb])
        S.dma("sp", U[:], U_d[:, :], writes=[identb])
        ones = cx.sb(es, "ones", [128, 128], F32)
        onesb = Buf()
        S.op("dve", lambda e: e.memset(ones[:], 1.0), writes=[onesb])
        S.barrier()
        modT, modb = stage_ada(cx, es, c_in, w_ada, b_ada, ident)
        gsT = cx.sb(es, "gsT", [128, 3 * DC], F32)
        gsb = Buf()
        for s in (1, 4, 7):
            S.op("dve", lambda e: e.tensor_scalar(out=modT[:, s * DC:(s + 1) * DC], in0=modT[:, s * DC:(s + 1) * DC],
                                                  scalar1=1.0, scalar2=None, op0=ALU.add), reads=[modb], writes=[modb])
        for i, (s, f) in enumerate(((2, 0.5), (5, 1.0), (8, 0.5))):
            S.op("dve", lambda e: e.tensor_scalar(out=gsT[:, i * DC:(i + 1) * DC], in0=modT[:, s * DC:(s + 1) * DC],
                                                  scalar1=float(f), scalar2=None, op0=ALU.mult), reads=[modb], writes=[gsb])
        lnv = {}
        lnbuf = Buf()
        for k in lnp:
            t, b = load_vec_fm(cx, es, k, lnp[k], D, ident)
            lnv[k] = t
        S.barrier()
        cx.wbufs = {}
        wsrc = [("ffn1_w_gate", Wg1, D, FF), ("ffn1_w_up", Wu1, D, FF), ("ffn1_w_down", Wd1, FF, D)]
        if stages != "ffn1":
            wsrc += [("w_in", w_in, D, IN_DIM)]
        if stages == "all":
            wsrc += [("w_branch_a", Wa, D, D), ("w_branch_b", Wb, D, D), ("w_out", Wo, D, D),
                     ("ffn2_w_gate", Wg2, D, FF), ("ffn2_w_up", Wu2, D, FF), ("ffn2_w_down", Wd2, FF, D)]
        hW = {}
        mats = []
        for (nm, ap_, K_, N_) in wsrc:
            hd = dt("bf_" + nm, [K_, N_], BF16, "Internal")
            cx.wbufs[hd.tensor.name] = Buf(nm)
            hW[nm] = hd
            mats.append((ap_, hd, K_, N_))
        precast_stage(cx, mats)
        Wg1, Wu1, Wd1 = hW["ffn1_w_gate"], hW["ffn1_w_up"], hW["ffn1_w_down"]
        if "w_in" in hW:
            w_in = hW["w_in"]
        if "w_out" in hW:
            Wa, Wb, Wo = hW["w_branch_a"], hW["w_branch_b"], hW["w_out"]
            Wg2, Wu2, Wd2 = hW["ffn2_w_gate"], hW["ffn2_w_up"], hW["ffn2_w_down"]
        mk = lambda: [Buf() for _ in range(NT)]
        xin_b, x1b, h1b, x2b, h2b, yb, tmb, fmb, oab, obb = (mk() for _ in range(10))
        only1 = (stages == "ffn1")
        ffn_stage(cx, "ffn1", S_len, xT, xin_b, None, None, modT, modb, 1 * DC, 0 * DC, Wg1, Wu1, Wd1,
                  gsT[:, 0:DC], gsb, lnv["ln1_g"], lnv["ln1_b"], lnbuf, 4 * DC, 3 * DC,
                  yT if only1 else x1T, yb if only1 else x1b, h1T, h1b, ones, onesb)
        if not only1:
            inproj_stage(cx, S_len, h1T, h1b, w_in, proj_tm, tmb, proj_fm, fmb)
            ssd_stage(cx, S_len, proj_tm, tmb, proj_fm, fmb, conv_w, conv_b, dt_bias, a_log, d_skip, norm_w,
                      xbc_c, obT, obb, ident, ones, U)
            if stages == "ssd":
                dbg_tm = dt("dbg_tm", [S_len, TM_W], F32, "ExternalOutput")
                dbg_xbc = dt("dbg_xbc", [CONV_CH, S_len], F32, "ExternalOutput")
                dbg_fm = dt("dbg_fm", [FM_H, S_len], F32, "ExternalOutput")
                S.dma("pool", dbg_xbc[:, 0:128], xbc_c[:, 0:128], reads=obb)
                with cx.scope() as tes:
                    tb16 = cx.sb(tes, "dbg16", [128, DC, TT], BF16)
                    tf32 = cx.sb(tes, "dbg32", [128, DC, TT], F32)
                    db = Buf()
                    for ti in range(NT):
                        S.dma("pool", tb16[:], obT.rearrange("(c p) t -> p c t", p=128)[:, :, ti * TT:(ti + 1) * TT], reads=[obb[ti]], writes=[db])
                        S.op("dve", lambda e: e.tensor_copy(out=tf32[:], in_=tb16[:]), reads=[db], writes=[db])
                        S.dma("pool", yT.rearrange("(c p) t -> p c t", p=128)[:, :, ti * TT:(ti + 1) * TT], tf32[:], reads=[db], writes=[yb[ti]])
            else:
                prep = nsa_prep_stage(cx, S_len, proj_tm, tmb, proj_fm, fmb, positions, cmp_pos, kw1, kw2, vw1, vw2,
                                      qT, ksT, kwT, v_tm, KcT, Vc_tm, ident, C)
                nsa_attn_stage(cx, S_len, prep, proj_fm, fmb, qT, ksT, kwT, v_tm, KcT, Vc_tm, oaT, oab, ident, ones, C)
                mg = dict(oaT=oaT, obT=obT, oa_bufs=oab, ob_bufs=obb, proj_fm=proj_fm, fm_bufs=fmb)
                ffn_stage(cx, "merge", S_len, x1T, x1b, None, None, modT, modb, 0, 0, Wa, Wb, Wo,
                          gsT[:, DC:2 * DC], gsb, lnv["ln2_g"], lnv["ln2_b"], lnbuf, 7 * DC, 6 * DC,
                          x2T, x2b, h2T, h2b, ones, onesb, merge=mg)
                ffn_stage(cx, "ffn2", S_len, x2T, x2b, h2T, h2b, modT, modb, 0, 0, Wg2, Wu2, Wd2,
                          gsT[:, 2 * DC:3 * DC], gsb, lnv["ln3_g"], lnv["ln3_b"], lnbuf, 0, 0,
                          yT, yb, None, None, ones, onesb)
        S.barrier()
    return nc


TWO_PI = 6.283185307179586
RC1 = 6.28125
RC2 = TWO_PI - RC1
PI = 3.141592653589793
ATT_SCALE = 128.0 ** -0.5
MASK_NEG = -30000.0


def rope_cs(cx, n, posi, pb, W):
    S = cx.S
    posf, ang, ki, kf, r, m, cs = W["posf"], W["ang"], W["ki"], W["kf"], W["r"], W["m"], W["cs"]
    wb = W["buf"]
    S.op("dve", lambda e: e.tensor_copy(out=posf[0:n, :], in_=posi[0:n, :]), reads=[pb], writes=[wb])
    S.op("dve", lambda e: e.scalar_tensor_tensor(out=ang[0:n, :], in0=W["invf2"][0:n, :], scalar=posf[0:n, 0:1],
                                                 in1=W["offs"][0:n, :], op0=ALU.mult, op1=ALU.add), reads=[wb], writes=[wb])
    S.op("dve", lambda e: e.tensor_scalar(out=ki[0:n, :], in0=ang[0:n, :], scalar1=1.0 / TWO_PI, scalar2=None, op0=ALU.mult),
         reads=[wb], writes=[wb])
    S.op("dve", lambda e: e.tensor_copy(out=kf[0:n, :], in_=ki[0:n, :]), reads=[wb], writes=[wb])
    S.op("dve", lambda e: e.scalar_tensor_tensor(out=r[0:n, :], in0=kf[0:n, :], scalar=-RC1, in1=ang[0:n, :],
                                                 op0=ALU.mult, op1=ALU.add), reads=[wb], writes=[wb])
    S.op("dve", lambda e: e.scalar_tensor_tensor(out=r[0:n, :], in0=kf[0:n, :], scalar=-RC2, in1=r[0:n, :],
                                                 op0=ALU.mult, op1=ALU.add), reads=[wb], writes=[wb])
    S.op("dve", lambda e: e.tensor_scalar(out=m[0:n, :], in0=r[0:n, :], scalar1=PI, scalar2=-TWO_PI, op0=ALU.is_gt, op1=ALU.mult),
         reads=[wb], writes=[wb])
    S.op("dve", lambda e: e.tensor_tensor(out=r[0:n, :], in0=r[0:n, :], in1=m[0:n, :], op=ALU.add), reads=[wb], writes=[wb])
    S.op("dve", lambda e: e.tensor_scalar(out=m[0:n, :], in0=r[0:n, :], scalar1=-PI, scalar2=TWO_PI, op0=ALU.is_lt, op1=ALU.mult),
         reads=[wb], writes=[wb])
    S.op("dve", lambda e: e.tensor_tensor(out=r[0:n, :], in0=r[0:n, :], in1=m[0:n, :], op=ALU.add), reads=[wb], writes=[wb])
    S.op("dve", lambda e: e.tensor_scalar(out=r[0:n, :], in0=r[0:n, :], scalar1=PI, scalar2=-PI, op0=ALU.min, op1=ALU.max),
         reads=[wb], writes=[wb])
    S.op("act", lambda e: e.activation(out=cs[0:n, :], in_=r[0:n, :], func=AF.Sin), reads=[wb], writes=[wb])
    return cs, wb


def rope_work(cx, es, invf2_d, offs_d):
    W = {}
    for k, shp, t in (("posf", [128, 1], F32), ("ang", [128, 32], F32), ("ki", [128, 32], I32), ("kf", [128, 32], F32),
                      ("r", [128, 32], F32), ("m", [128, 32], F32), ("cs", [128, 32], F32),
                      ("invf2", [128, 32], F32), ("offs", [128, 32], F32)):
        W[k] = cx.sb(es, "rw_" + k, shp, t)
    W["buf"] = Buf()
    cx.S.dma("sp", W["invf2"][:], invf2_d[:, :], writes=[W["buf"]])
    cx.S.dma("sp", W["offs"][:], offs_d[:, :], writes=[W["buf"]])
    for k in ("ta", "tb", "tc", "td"):
        W[k] = cx.sb(es, "rw_" + k, [128, 40, 16], F32)
    return W


def rope_apply(cx, n, X, H, xb, cs, csb, W):
    S = cx.S
    cosb = cs[0:n, 16:32].unsqueeze(1).to_broadcast([n, H, 16])
    sinb = cs[0:n, 0:16].unsqueeze(1).to_broadcast([n, H, 16])
    t1, t2 = X[:, :, 0:16], X[:, :, 16:32]
    ta, tb, tc, td = (W[k][0:n, 0:H, :] for k in ("ta", "tb", "tc", "td"))
    rb = W["buf"]
    S.op("dve", lambda e: e.tensor_tensor(out=ta, in0=t1, in1=cosb, op=ALU.mult), reads=[xb, csb], writes=[rb])
    S.op("dve", lambda e: e.tensor_tensor(out=tb, in0=t2, in1=sinb, op=ALU.mult), reads=[xb, csb], writes=[rb])
    S.op("dve", lambda e: e.tensor_tensor(out=tc, in0=t2, in1=cosb, op=ALU.mult), reads=[xb, csb], writes=[rb])
    S.op("dve", lambda e: e.tensor_tensor(out=td, in0=t1, in1=sinb, op=ALU.mult), reads=[xb, csb], writes=[rb])
    S.op("dve", lambda e: e.tensor_tensor(out=t1, in0=ta, in1=tb, op=ALU.subtract), reads=[rb], writes=[xb])
    S.op("dve", lambda e: e.tensor_tensor(out=t2, in0=tc, in1=td, op=ALU.add), reads=[rb], writes=[xb])


def nsa_prep_stage(cx, S_len, proj_tm, tm_bufs, proj_fm, fm_bufs, positions, cmp_pos, kw1, kw2, vw1, vw2,
                   qT, ksT, kwT, v_tm, KcT, Vc_tm, ident, C):
    S = cx.S
    done = Buf("nsa_prep")
    alltm, allfm = list(tm_bufs), list(fm_bufs)
    NCP = S_len // 16
    n_cmp = NCP - 1
    pos_v = positions.rearrange("(t o) -> t o", o=1)
    pos16 = positions.rearrange("(c r) -> c r", r=16)
    qT_v = qT.rearrange("(h d) t -> d h t", d=128)
    ksT_v = ksT.rearrange("(h d) t -> d h t", d=128)
    kwT_v = kwT.rearrange("(h d) t -> d h t", d=128)
    with cx.scope() as es:
        W = rope_work(cx, es, C["invf2"], C["offs"])
        qk = cx.sb(es, "qk", [128, 40 * 128], F32); qkb = Buf()
        qkT = cx.sb(es, "qkT", [128, 40, 128], BF16); qkTb = Buf()
        vv = cx.sb(es, "vv", [128, 1024], F32); vvb = Buf()
        vh = cx.sb(es, "vh", [128, 1024], BF16)
        posi = cx.sb(es, "posi", [128, 1], I32); pb = Buf()
        for st in range(S_len // 128):
            t0 = st * 128
            S.dma("pool", posi[:], pos_v[t0:t0 + 128, :], writes=[pb])
            cs, csb = rope_cs(cx, 128, posi, pb, W)
            S.dma("pool", qk[:], proj_tm[t0:t0 + 128, TM_Q:TM_Q + 5120], reads=alltm, writes=[qkb])
            S.dma("pool", vv[:], proj_tm[t0:t0 + 128, TM_VS:TM_VS + 1024], reads=alltm, writes=[vvb])
            rope_apply(cx, 128, qk[:].rearrange("p (h d) -> p h d", d=128), 40, qkb, cs, csb, W)
            for q4 in range(10):
                pt, ptb = cx.bank()
                for j in range(4):
                    hh = q4 * 4 + j
                    S.op("pe", lambda e: e.transpose(pt[:, j * 128:(j + 1) * 128], qk[:, hh * 128:(hh + 1) * 128], ident[:]),
                         reads=[qkb], writes=[ptb])
                S.op("act", lambda e: e.activation(out=qkT[:, q4 * 4:(q4 + 1) * 4, :],
                                                   in_=pt[:].rearrange("p (j n) -> p j n", j=4), func=AF.Copy),
                     reads=[ptb], writes=[qkTb])
            S.dma("pool", qT_v[:, :, t0:t0 + 128], qkT[:, 0:32, :], reads=[qkTb], writes=[done])
            S.dma("pool", ksT_v[:, :, t0:t0 + 128], qkT[:, 32:36, :], reads=[qkTb], writes=[done])
            S.dma("pool", kwT_v[:, :, t0:t0 + 128], qkT[:, 36:40, :], reads=[qkTb], writes=[done])
            S.op("pool", lambda e: e.tensor_copy(out=vh[:], in_=vv[:]), reads=[vvb], writes=[vvb])
            S.dma("pool", v_tm[t0:t0 + 128, :], vh[:], reads=[vvb], writes=[done])
    with cx.scope() as es:
        W = rope_work(cx, es, C["invf2"], C["offs"])
        posT, posTb = load_vec_fm(cx, es, "posT", cmp_pos.rearrange("l d -> (l d)"), 32 * 128, ident)
        posTh = cx.sb(es, "posTh", [128, 32], BF16)
        S.op("dve", lambda e: e.tensor_copy(out=posTh[:], in_=posT[:]), reads=[posTb], writes=[posTb])
        X = cx.sb(es, "X", [128, S_len + 16], F32); Xb = Buf()
        Xh = cx.sb(es, "Xh", [128, S_len + 16], BF16)
        w1s = cx.sb(es, "w1s", [128, 32, 256], F32); w1b = Buf()
        w1h = cx.sb(es, "w1h", [128, 32, 256], BF16)
        w2s = cx.sb(es, "w2s", [128, 2, 128], F32); w2b = Buf()
        w2h = cx.sb(es, "w2h", [128, 2, 128], BF16)
        sT = cx.sb(es, "sT", [128, 2, NCP], BF16); sTb = Buf()
        bia = cx.sb(es, "bia", [128, 2], F32); biab = Buf()
        kc = cx.sb(es, "kc", [128, 128], F32); kcb = Buf()
        kcT = cx.sb(es, "kcT", [128, 128], BF16); kcTb = Buf()
        vch = cx.sb(es, "vch", [128, 128], BF16); vchb = Buf()
        posi = cx.sb(es, "posi", [128, 1], I32); pb = Buf()
        X3 = Xh[:].rearrange("p (c r) -> p c r", r=16)
        for kind, (w1, w2, row0) in enumerate(((kw1, kw2, FM_KC), (vw1, vw2, FM_VC))):
            S.dma("sp", w1s[:], w1.rearrange("(l d) n -> d l n", d=128), writes=[w1b])
            S.op("pool", lambda e: e.tensor_copy(out=w1h[:], in_=w1s[:]), reads=[w1b], writes=[w1b])
            S.dma("sp", w2s[:], w2.rearrange("(a p) n -> p a n", p=128), writes=[w2b])
            S.op("pool", lambda e: e.tensor_copy(out=w2h[:], in_=w2s[:]), reads=[w2b], writes=[w2b])
            pbk, pbkb = cx.bank()
            for hc in range(2):
                for l in range(32):
                    S.op("pe", lambda e: e.matmul(pbk[:, hc:hc + 1], lhsT=w1h[:, l, hc * 128:(hc + 1) * 128], rhs=posTh[:, l:l + 1],
                                                  start=(l == 0), stop=(l == 31)), reads=[w1b, posTb], writes=[pbkb])
            S.op("dve", lambda e: e.tensor_copy(out=bia[:], in_=pbk[:, 0:2]), reads=[pbkb], writes=[biab])
            for g in range(NSA_G):
                S.op("dve", lambda e: e.memset(X[:, S_len:S_len + 16], 0.0), writes=[Xb])
                S.dma("pool", X[:, 0:S_len], proj_fm[row0 + g * 128:row0 + (g + 1) * 128, :], reads=allfm, writes=[Xb])
                S.op("pool", lambda e: e.tensor_copy(out=Xh[:], in_=X[:]), reads=[Xb], writes=[Xb])
                for hc in range(2):
                    for c0 in range(0, NCP, 512):
                        ncl = min(512, NCP - c0)
                        ph, phb = cx.bank()
                        for l in range(32):
                            rhs = X3[:, c0 + l // 16:c0 + l // 16 + ncl, l % 16]
                            S.op("pe", lambda e: e.matmul(ph[:, 0:ncl], lhsT=w1h[:, l, hc * 128:(hc + 1) * 128], rhs=rhs,
                                                          start=(l == 0), stop=(l == 31)), reads=[w1b, Xb], writes=[phb])
                        S.op("act", lambda e: e.activation(out=sT[:, hc, c0:c0 + ncl], in_=ph[:, 0:ncl], func=AF.Silu,
                                                           bias=bia[:, hc:hc + 1]), reads=[phb, biab], writes=[sTb])
                for ct in range(NCP // 128):
                    c0 = ct * 128
                    po, pob = cx.bank()
                    for hc in range(2):
                        S.op("pe", lambda e: e.matmul(po[:, 0:128], lhsT=sT[:, hc, c0:c0 + 128], rhs=w2h[:, hc, :],
                                                      start=(hc == 0), stop=(hc == 1)), reads=[sTb, w2b], writes=[pob])
                    if kind == 0:
                        n = min(128, n_cmp - c0)
                        S.op("dve", lambda e: e.memset(posi[:], 0), writes=[pb])
                        S.dma("pool", posi[0:n, :], pos16[c0 + 1:c0 + 1 + n, 15:16], writes=[pb], allow_slow_non_contiguous=True)
                        cs, csb = rope_cs(cx, 128, posi, pb, W)
                        S.op("dve", lambda e: e.tensor_copy(out=kc[:], in_=po[:, 0:128]), reads=[pob], writes=[kcb])
                        rope_apply(cx, 128, kc[:].rearrange("p (h d) -> p h d", d=128), 1, kcb, cs, csb, W)
                        pt, ptb = cx.bank()
                        S.op("pe", lambda e: e.transpose(pt[:, 0:128], kc[:], ident[:]), reads=[kcb], writes=[ptb])
                        S.op("act", lambda e: e.activation(out=kcT[:], in_=pt[:, 0:128], func=AF.Copy), reads=[ptb], writes=[kcTb])
                        S.dma("pool", KcT[g * 128:(g + 1) * 128, c0:c0 + 128], kcT[:], reads=[kcTb], writes=[done])
                    else:
                        S.op("act", lambda e: e.activation(out=vch[:], in_=po[:, 0:128], func=AF.Copy), reads=[pob], writes=[vchb])
                        S.dma("pool", Vc_tm[c0:c0 + 128, g * 128:(g + 1) * 128], vch[:], reads=[vchb], writes=[done])
    return done


def nsa_attn_stage(cx, S_len, prep, proj_fm, fm_bufs, qT, ksT, kwT, v_tm, KcT, Vc_tm, oaT, oa_bufs, ident, ones, C):
    S = cx.S
    NCP = S_len // 16
    NCT = NCP // 128
    NKT = S_len // 128
    NQG = S_len // TT
    allfm = list(fm_bufs)
    qT_v = qT.rearrange("(h d) t -> d h t", d=128)
    oa_v = oaT.rearrange("(h d) t -> d h t", d=128)
    with cx.scope() as es:
        cb = Buf("consts")
        identh = cx.sb(es, "identh", [128, 128], BF16)
        onesh = cx.sb(es, "onesh", [128, 128], BF16)
        S.op("dve", lambda e: e.tensor_copy(out=identh[:], in_=ident[:]), writes=[cb])
        S.op("dve", lambda e: e.memset(onesh[:], 1.0), writes=[cb])
        Esel = cx.sb(es, "Esel", [128, S_len], BF16)
        winM = cx.sb(es, "winM", [128, 8, TT], BF16)
        cauM = cx.sb(es, "cauM", [128, 4, TT], BF16)
        ovl = cx.sb(es, "ovl", [128, NCT, 129], BF16)
        S.dma("sp", Esel[:], C["Esel"][:, 0:S_len], writes=[cb])
        S.dma("sp", winM[:], C["winM"][:, :, :], writes=[cb])
        S.dma("sp", cauM[:], C["cauM"][:, :, :], writes=[cb])
        S.dma("sp", ovl[:], C["ovl1"].rearrange("(a p) n -> p a n", p=128), writes=[cb])
        S.barrier()
        KsT = cx.sb(es, "KsT", [128, S_len], BF16)
        KwT = cx.sb(es, "KwT", [128, S_len], BF16)
        Vs = cx.sb(es, "Vs", [128, NKT, 128], BF16)
        Vw = cx.sb(es, "Vw", [128, NKT, 128], BF16)
        Kc = cx.sb(es, "Kc", [128, NCP], BF16)
        Vc = cx.sb(es, "Vc", [128, NCT, 128], BF16)
        kvb = Buf("kv")
        Q = cx.sb(es, "Q", [128, 8, TT], BF16); Qb = Buf()
        gT = cx.sb(es, "gT", [96, TT], F32); gTb = Buf()
        gsel = cx.sb(es, "gsel", [96, TT], F32); gselb = Buf()
        mC = [cx.sb(es, "mC", [128, TT], BF16) for _ in range(2)]; mCb = Buf()
        amask = cx.sb(es, "amask", [128, 4, 128], F32); amb = Buf()
        acc = cx.sb(es, "acc", [128, 8, TT], F32); accb = [Buf() for _ in range(8)]
        acch = cx.sb(es, "acch", [128, 8, TT], BF16); acchb = Buf()
        PT = [cx.sb(es, "PT", [128, TT], BF16) for _ in range(3)]; PTb = [Buf() for _ in range(3)]
        PTc = cx.sb(es, "PTc", [128, NCT, TT], BF16); PTcb = Buf()
        impa = cx.sb(es, "impa", [128, 4, 128], F32); impb_ = Buf()
        rd = cx.sb(es, "rd", [128, 4], F32); rdb = Buf()
        sc = cx.sb(es, "sc", [128, 128], F32); scb = Buf()
        sc2 = cx.sb(es, "sc2", [128, 128], F32)
        m8 = cx.sb(es, "m8", [128, 8], F32)
        biasT = cx.sb(es, "biasT", [128, TT], BF16); biasb = Buf()
        rden = cx.sb(es, "rden", [128, TT], F32); rdenb = Buf()
        wgt = cx.sb(es, "wgt", [128, TT], F32); wgtb = Buf()
        tmpo = cx.sb(es, "tmpo", [128, TT], F32); tmpob = Buf()
        pi = [0]

        def combine(hl, r, Ob, Db, first):
            S.op("dve", lambda e: e.tensor_scalar(out=gsel[:], in0=gT[:], scalar1=ident[0:96, r:r + 1], scalar2=None, op0=ALU.mult),
                 reads=[gTb], writes=[gselb])
            pg, pgb = cx.bank()
            S.op("pe", lambda e: e.matmul(pg[:], lhsT=ones[0:96, :], rhs=gsel[:], start=True, stop=True), reads=[gselb], writes=[pgb])
            S.op("dve", lambda e: e.tensor_scalar(out=rden[:], in0=Db[0][:], scalar1=1e-30, scalar2=None, op0=ALU.max),
                 reads=[Db[1]], writes=[rdenb])
            S.op("dve", lambda e: e.reciprocal(out=rden[:], in_=rden[:]), reads=[rdenb], writes=[rdenb])
            S.op("dve", lambda e: e.tensor_tensor(out=wgt[:], in0=rden[:], in1=pg[:], op=ALU.mult), reads=[rdenb, pgb], writes=[wgtb])
            if first:
                S.op("dve", lambda e: e.tensor_tensor(out=acc[:, hl, :], in0=wgt[:], in1=Ob[0][:], op=ALU.mult),
                     reads=[wgtb, Ob[1]], writes=[accb[hl]])
            else:
                S.op("dve", lambda e: e.tensor_tensor(out=tmpo[:], in0=wgt[:], in1=Ob[0][:], op=ALU.mult),
                     reads=[wgtb, Ob[1]], writes=[tmpob])
                S.op("dve", lambda e: e.tensor_tensor(out=acc[:, hl, :], in0=acc[:, hl, :], in1=tmpo[:], op=ALU.add),
                     reads=[tmpob, accb[hl]], writes=[accb[hl]])

        def attend(hl, tiles, Ob, Db, keep=None):
            nt = len(tiles)
            for i, (kl, vl, extra) in enumerate(tiles):
                ps, psb = cx.bank()
                S.op("pe", lambda e: e.matmul(ps[:], lhsT=kl, rhs=Q[:, hl, :], start=True, stop=(len(extra) == 0)),
                     reads=[kvb, Qb], writes=[psb])
                for xi, (ml, mr, mb) in enumerate(extra):
                    S.op("pe", lambda e: e.matmul(ps[:], lhsT=ml, rhs=mr, start=False, stop=(xi == len(extra) - 1)),
                         reads=[cb] + mb, writes=[psb])
                if keep is None:
                    k = pi[0] % 3
                    pi[0] += 1
                    P, Pb = PT[k], PTb[k]
                    Pap = P[:]
                else:
                    Pap, Pb = keep[:, i, :], PTcb
                S.op("act", lambda e: e.activation(out=Pap, in_=ps[:], func=AF.Exp, scale=float(ATT_SCALE)), reads=[psb], writes=[Pb])
                S.op("pe", lambda e: e.matmul(Ob[0][:], lhsT=vl, rhs=Pap, start=(i == 0), stop=(i == nt - 1)),
                     reads=[kvb, Pb], writes=[Ob[1]])
                S.op("pe", lambda e: e.matmul(Db[0][:], lhsT=onesh[:], rhs=Pap, start=(i == 0), stop=(i == nt - 1)),
                     reads=[cb, Pb], writes=[Db[1]])

        for g in range(NSA_G):
            S.dma("pool", KsT[:], ksT[g * 128:(g + 1) * 128, :], reads=[prep], writes=[kvb])
            S.dma("pool", KwT[:], kwT[g * 128:(g + 1) * 128, :], reads=[prep], writes=[kvb])
            S.dma("pool", Vs[:], v_tm[:, g * 128:(g + 1) * 128].rearrange("(a p) d -> p a d", p=128), reads=[prep], writes=[kvb])
            S.dma("pool", Vw[:], v_tm[:, 512 + g * 128:512 + (g + 1) * 128].rearrange("(a p) d -> p a d", p=128), reads=[prep], writes=[kvb])
            S.dma("pool", Kc[:], KcT[g * 128:(g + 1) * 128, :], reads=[prep], writes=[kvb])
            S.dma("pool", Vc[:], Vc_tm[:, g * 128:(g + 1) * 128].rearrange("(a p) d -> p a d", p=128), reads=[prep], writes=[kvb])
            for qi in range(NQG):
                q0 = qi * TT
                S.dma("pool", Q[:], qT_v[:, g * 8:(g + 1) * 8, q0:q0 + TT], reads=[prep], writes=[Qb])
                S.dma("pool", gT[:], proj_fm[FM_GN:FM_GN + 96, q0:q0 + TT], reads=allfm, writes=[gTb])
                S.op("act", lambda e: e.activation(out=gT[:], in_=gT[:], func=AF.Sigmoid), reads=[gTb], writes=[gTb])
                S.dma("pool", amask[:], C["addmask"][q0:q0 + TT, :].rearrange("(a p) j -> p a j", p=128), writes=[amb])
                nct = min(NCT, ((q0 + 480) // 16) // 128 + 1)
                partial = [ct for ct in range(nct) if 16 * (ct * 128 + 127) + 31 > q0]
                assert len(partial) <= 2
                mct = {}
                for k, ct in enumerate(partial):
                    S.dma("pool", mC[k][:], C["maskC"][ct * 128:(ct + 1) * 128, q0:q0 + TT], writes=[mCb])
                    mct[ct] = mC[k]
                for hl in range(8):
                    Ob, Db = cx.bank(True), cx.bank(True)
                    tiles = []
                    for ct in range(nct):
                        extra = [(identh[:], mct[ct][:], [mCb])] if ct in mct else []
                        tiles.append((Kc[:, ct * 128:(ct + 1) * 128], Vc[:, ct, :], extra))
                    attend(hl, tiles, Ob, Db, keep=PTc)
                    ib = [cx.bank(True), cx.bank(True)]
                    for qs in range(4):
                        pt, ptb = ib[qs // 2]
                        co = (qs % 2) * 129
                        for ct in range(nct):
                            S.op("pe", lambda e: e.matmul(pt[:, co:co + 129], lhsT=PTc[:, ct, qs * 128:(qs + 1) * 128], rhs=ovl[:, ct, :],
                                                          start=(ct == 0), stop=(ct == nct - 1)), reads=[PTcb, cb], writes=[ptb])
                    for qs in range(4):
                        pt, ptb = ib[qs // 2]
                        co = (qs % 2) * 129
                        S.op("dve", lambda e: e.tensor_scalar(out=rd[:, qs:qs + 1], in0=pt[:, co + 128:co + 129], scalar1=1e-30, scalar2=None,
                                                              op0=ALU.max), reads=[ptb], writes=[rdb])
                        S.op("dve", lambda e: e.reciprocal(out=rd[:, qs:qs + 1], in_=rd[:, qs:qs + 1]), reads=[rdb], writes=[rdb])
                        if hl == 0:
                            S.op("dve", lambda e: e.tensor_scalar(out=impa[:, qs, :], in0=pt[:, co:co + 128], scalar1=rd[:, qs:qs + 1],
                                                                  scalar2=None, op0=ALU.mult), reads=[ptb, rdb], writes=[impb_])
                        else:
                            S.op("dve", lambda e: e.scalar_tensor_tensor(out=impa[:, qs, :], in0=pt[:, co:co + 128], scalar=rd[:, qs:qs + 1],
                                                                         in1=impa[:, qs, :], op0=ALU.mult, op1=ALU.add),
                                 reads=[ptb, rdb, impb_], writes=[impb_])
                    combine(hl, 0 * 32 + g * 8 + hl, Ob, Db, True)
                    cx.release(Ob, Db, ib[0], ib[1])
                pb_, pbb = cx.bank()
                for qs in range(4):
                    S.op("dve", lambda e: e.tensor_tensor(out=sc[:], in0=impa[:, qs, :], in1=amask[:, qs, :], op=ALU.add),
                         reads=[impb_, amb], writes=[scb])
                    S.op("dve", lambda e: e.max(out=m8[:], in_=sc[:]), reads=[scb], writes=[scb])
                    S.op("dve", lambda e: e.match_replace(out=sc2[:], in_to_replace=m8[:], in_values=sc[:], imm_value=-3.0e38),
                         reads=[scb], writes=[scb])
                    S.op("dve", lambda e: e.max(out=m8[:], in_=sc2[:]), reads=[scb], writes=[scb])
                    S.op("dve", lambda e: e.tensor_scalar(out=sc2[:], in0=sc[:], scalar1=m8[:, 7:8], scalar2=-1.0,
                                                          op0=ALU.is_ge, op1=ALU.add), reads=[scb], writes=[scb])
                    S.op("pe", lambda e: e.transpose(pb_[:, qs * 128:(qs + 1) * 128], sc2[:], ident[:]), reads=[scb], writes=[pbb])
                S.op("act", lambda e: e.activation(out=biasT[:], in_=pb_[:], func=AF.Copy, scale=float(-MASK_NEG)),
                     reads=[pbb], writes=[biasb])
                kt_hi = (q0 + TT - 1) // 128
                for hl in range(8):
                    Ob, Db = cx.bank(True), cx.bank(True)
                    tiles = []
                    for kt in range(kt_hi + 1):
                        extra = [(Esel[:, kt * 128:(kt + 1) * 128], biasT[:], [biasb])]
                        if kt * 128 >= q0:
                            extra.append((identh[:], cauM[:, kt - 4 * qi, :], []))
                        tiles.append((KsT[:, kt * 128:(kt + 1) * 128], Vs[:, kt, :], extra))
                    attend(hl, tiles, Ob, Db)
                    combine(hl, 1 * 32 + g * 8 + hl, Ob, Db, False)
                    cx.release(Ob, Db)
                    Ob, Db = cx.bank(True), cx.bank(True)
                    tiles = []
                    for r in range(8):
                        kt = 4 * qi - 4 + r
                        if kt < 0:
                            continue
                        tiles.append((KwT[:, kt * 128:(kt + 1) * 128], Vw[:, kt, :], [(identh[:], winM[:, r, :], [])]))
                    attend(hl, tiles, Ob, Db)
                    combine(hl, 2 * 32 + g * 8 + hl, Ob, Db, False)
                    cx.release(Ob, Db)
                S.op("act", lambda e: e.activation(out=acch[:], in_=acc[:], func=AF.Copy), reads=accb, writes=[acchb])
                S.dma("pool", oa_v[:, g * 8:(g + 1) * 8, q0:q0 + TT], acch[:], reads=[acchb], writes=[oa_bufs[qi]])


def nsa_stub_stage(cx, S_len, oaT, oa_bufs):
    S = cx.S
    with cx.scope() as es:
        zt = cx.sb(es, "zt", [128, DC, TT], BF16)
        zb = Buf()
        S.op("dve", lambda e: e.memset(zt[:], 0.0), writes=[zb])
        v = oaT.rearrange("(c p) t -> p c t", p=128)
        for ti in range(S_len // TT):
            S.dma("pool", v[:, :, ti * TT:(ti + 1) * TT], zt[:], reads=[zb], writes=[oa_bufs[ti]])


def nsa_consts(S_len):
    import ml_dtypes
    bf = ml_dtypes.bfloat16
    NCP = S_len // 16
    half = 16
    invf = (np.float32(500000.0) ** (-np.arange(half, dtype=np.float32) / np.float32(half))).astype(np.float32)
    invf2 = np.tile(np.concatenate([invf, invf])[None, :], (128, 1)).astype(np.float32)
    offs = np.tile(np.concatenate([np.zeros(16, np.float32), np.full(16, np.pi / 2, np.float32)])[None, :], (128, 1)).astype(np.float32)
    key = np.arange(S_len)
    esel = (key[None, :] // 64 == np.arange(128)[:, None]).astype(np.float32).astype(bf)
    p = np.arange(128)[:, None, None]
    r8 = np.arange(8)[None, :, None]
    q = np.arange(TT)[None, None, :]
    kp = r8 * 128 + p
    winm = np.where((kp - 512 <= q) & (kp > q), 0.0, MASK_NEG).astype(np.float32).astype(bf)
    r4 = np.arange(4)[None, :, None]
    caum = np.where(r4 * 128 + p <= q, 0.0, MASK_NEG).astype(np.float32).astype(bf)
    c = np.arange(NCP)
    j = np.arange(128)
    ovl = ((16 * c[:, None] < 64 * j[None, :] + 64) & (16 * c[:, None] + 31 >= 64 * j[None, :])).astype(np.float32)
    ovl[NCP - 1, :] = 0.0
    ovl1 = np.concatenate([ovl, np.ones((NCP, 1), np.float32)], axis=1).astype(bf)
    t = np.arange(S_len)
    maskc = np.where(16 * c[:, None] + 31 <= t[None, :], 0.0, MASK_NEG).astype(np.float32).astype(bf)
    cur = t // 64
    forced = (j[None, :] == 0) | (j[None, :] == cur[:, None]) | (j[None, :] == cur[:, None] - 1)
    causal = (64 * j[None, :] <= t[:, None])
    addmask = np.where(causal, np.where(forced, 1.0e4, 0.0), -1.0e30).astype(np.float32)
    return {"c_invf2": invf2, "c_offs": offs, "c_esel": np.ascontiguousarray(esel), "c_winm": np.ascontiguousarray(winm),
            "c_caum": np.ascontiguousarray(caum), "c_ovl1": np.ascontiguousarray(ovl1), "c_maskc": np.ascontiguousarray(maskc),
            "c_addmask": addmask}


INPUT_KEYS = ["nsa_cmp_pos", "nsa_cmp_k_w1", "nsa_cmp_k_w2", "nsa_cmp_v_w1", "nsa_cmp_v_w2", "c", "w_ada", "b_ada", "ffn1_w_gate", "ffn1_w_up", "ffn1_w_down", "w_in", "ssd_conv_w", "ssd_conv_b",
              "ssd_dt_bias", "ssd_a_log", "ssd_d", "ssd_norm_w", "w_branch_a", "w_branch_b", "w_out",
              "ffn2_w_gate", "ffn2_w_up", "ffn2_w_down", "ln1_g", "ln1_b", "ln2_g", "ln2_b", "ln3_g", "ln3_b"]


def make_in_maps(inputs, n_cores=8, batches=None):
    B = inputs["x"].shape[0]
    consts = {"ident": np.eye(128, dtype=np.float32), "Umat": np.triu(np.ones((128, 128), np.float32))}
    consts.update(nsa_consts(inputs["x"].shape[1]))
    shared = {}
    for k in INPUT_KEYS:
        a = np.asarray(inputs[k])
        if k != "c":
            a = a[0]
        shared[k] = np.ascontiguousarray(a, dtype=np.float32)
    maps = []
    for core in range(n_cores):
        b = (core * B) // n_cores if batches is None else batches[core]
        m = dict(shared)
        m["c"] = np.ascontiguousarray(shared["c"][b])
        m["xT"] = np.ascontiguousarray(np.asarray(inputs["x"])[b].T, dtype=np.float32)
        m["positions"] = np.ascontiguousarray(np.asarray(inputs["positions"])[b], dtype=np.int32)
        m.update(consts)
        maps.append(m)
    return maps


def kernel(**inputs):
    x = np.asarray(inputs["x"])
    B, S_len, _ = x.shape
    nc = build(S_len, "all")
    maps = make_in_maps(inputs)
    res = run_bass_kernel_spmd(nc, maps, core_ids=list(range(8)))
    out = np.empty((B, S_len, D), np.float32)
    for b in range(B):
        core = (b * 8) // B
        out[b] = res.results[core]["yT"].T
    return out


NSA_H, NSA_G, DH = 32, 4, 128
SSD_H, SSD_P, SSD_G, SSD_N = 64, 64, 8, 128
CONV_CH = D + 2 * SSD_G * SSD_N
C_Q, C_KC, C_VC, C_KS, C_VS, C_KW, C_VW, C_GN, C_Z, C_XBC, C_DT, C_GA, C_GB = (
    0, 4096, 4608, 5120, 5632, 6144, 6656, 7168, 7264, 11360, 17504, 17568, 21664)
IN_DIM = 25760
TM_SEGS = [(C_Q, 4096), (C_KS, 512), (C_KW, 512), (C_VS, 512), (C_VW, 512), (C_Z, 4096), (C_DT, 64)]
TM_Q, TM_KS, TM_KW, TM_VS, TM_VW, TM_Z, TM_DT = 0, 4096, 4608, 5120, 5632, 6144, 10240
TM_W = 10304
FM_SEGS = [(C_KC, 512), (C_VC, 512), (C_GN, 96), (C_XBC, CONV_CH), (C_GA, 4096), (C_GB, 4096)]
FM_KC, FM_VC, FM_GN, FM_XBC, FM_GA, FM_GB = 0, 512, 1024, 1152, 1152 + CONV_CH, 1152 + CONV_CH + 4096
FM_H = 1152 + CONV_CH + 8192


class RowSplit:
    def __init__(self, parts):
        self.parts = parts

    def __getitem__(self, idx):
        rs, cs = idx
        r0, r1 = rs.start, rs.stop
        for (p0, n, ap) in self.parts:
            if p0 <= r0 and r1 <= p0 + n:
                return ap[r0 - p0:r1 - p0, cs]
        raise IndexError((r0, r1))


class WLoader:
    def __init__(self, cx, es, nbf=8):
        self.cx = cx
        self.wbf = [cx.sb(es, "wbf", [128, 2, 512], BF16) for _ in range(nbf)]
        self.wbfb = [Buf() for _ in range(nbf)]
        self.i = 0

    def load(self, src_ap, ncols):
        S = self.cx.S
        i = self.i
        self.i += 1
        bf, bb = self.wbf[i % len(self.wbf)], self.wbfb[i % len(self.wbf)]
        S.dma("sp", bf[:, :, 0:ncols], src_ap.rearrange("(a p) n -> p a n", p=128),
              reads=[self.cx.wbufs[src_ap.tensor.name]], writes=[bb])
        return bf, bb


def precast_stage(cx, mats):
    S = cx.S
    with cx.scope() as es:
        NB = 3
        st = [cx.sb(es, "pcs", [128, 4096], F32) for _ in range(NB)]
        stb = [Buf() for _ in range(NB)]
        bf = [cx.sb(es, "pcb", [128, 4096], BF16) for _ in range(NB)]
        bfb = [Buf() for _ in range(NB)]
        i = 0
        for (src, dst, K_, N_) in mats:
            db = cx.wbufs[dst.tensor.name]
            for r0 in range(0, K_, 128):
                for c0 in range(0, N_, 4096):
                    n = min(4096, N_ - c0)
                    k = i % NB
                    S.dma("sp", st[k][:, 0:n], src[r0:r0 + 128, c0:c0 + n], writes=[stb[k]])
                    eng = ("act", "dve", "pool")[i % 3]
                    if eng == "act":
                        S.op("act", lambda e: e.activation(out=bf[k][:, 0:n], in_=st[k][:, 0:n], func=AF.Copy), reads=[stb[k]], writes=[bfb[k]])
                    else:
                        S.op(eng, lambda e: e.tensor_copy(out=bf[k][:, 0:n], in_=st[k][:, 0:n]), reads=[stb[k]], writes=[bfb[k]])
                    S.dma("pool", dst[r0:r0 + 128, c0:c0 + n], bf[k][:, 0:n], reads=[bfb[k]], writes=[db])
                    i += 1


def inproj_stage(cx, S_len, hT, h_bufs, w_in, proj_tm, tm_bufs, proj_fm, fm_bufs):
    S = cx.S
    NT = S_len // TT
    hs_v = hT.rearrange("(c p) t -> p c t", p=128)
    with cx.scope() as es:
        h = cx.sb(es, "h", [128, DC, TT], BF16)
        hb = Buf()
        wl = WLoader(cx, es)
        ost = [cx.sb(es, "ost", [128, 4, 512], F32) for _ in range(2)]
        ostb = [Buf() for _ in range(2)]
        oi = 0
        for ti in range(NT):
            t0 = ti * TT
            S.dma("pool", h[:], hs_v[:, :, t0:t0 + TT], reads=[h_bufs[ti]], writes=[hb])
            tcol = 0
            for (c0, n) in TM_SEGS:
                for g0 in range(0, n, 512):
                    nc_ = min(512, n - g0)
                    bk = [cx.bank() for _ in range(4)]
                    for k2 in range(DC // 2):
                        wt, wtb = wl.load(w_in[k2 * 256:(k2 + 1) * 256, c0 + g0:c0 + g0 + nc_], nc_)
                        for a in range(2):
                            kc = 2 * k2 + a
                            for ts in range(4):
                                S.op("pe", lambda e: e.matmul(bk[ts][0][:, 0:nc_], lhsT=h[:, kc, ts * 128:(ts + 1) * 128],
                                                              rhs=wt[:, a, 0:nc_], start=(kc == 0), stop=(kc == DC - 1)),
                                     reads=[wtb, hb], writes=[bk[ts][1]])
                    o, ob = ost[oi % 2], ostb[oi % 2]
                    oi += 1
                    for ts in range(4):
                        eng = "act" if ts % 2 == 0 else "dve"
                        if eng == "act":
                            S.op("act", lambda e: e.activation(out=o[:, ts, 0:nc_], in_=bk[ts][0][:, 0:nc_], func=AF.Copy),
                                 reads=[bk[ts][1]], writes=[ob])
                        else:
                            S.op("dve", lambda e: e.tensor_copy(out=o[:, ts, 0:nc_], in_=bk[ts][0][:, 0:nc_]),
                                 reads=[bk[ts][1]], writes=[ob])
                    dst = proj_tm[t0:t0 + TT, tcol + g0:tcol + g0 + nc_].rearrange("(a p) n -> p a n", p=128)
                    S.dma("pool", dst, o[:, :, 0:nc_], reads=[ob], writes=[tm_bufs[ti]])
                tcol += n
            frow = 0
            for (c0, n) in FM_SEGS:
                for g0 in range(0, n, 512):
                    nc_ = min(512, n - g0)
                    nf = (nc_ + 127) // 128
                    bk = [cx.bank() for _ in range(nf)]
                    for k2 in range(DC // 2):
                        wt, wtb = wl.load(w_in[k2 * 256:(k2 + 1) * 256, c0 + g0:c0 + g0 + nc_], nc_)
                        for a in range(2):
                            kc = 2 * k2 + a
                            for j in range(nf):
                                m = min(128, nc_ - j * 128)
                                S.op("pe", lambda e: e.matmul(bk[j][0][0:m, :], lhsT=wt[:, a, j * 128:j * 128 + m],
                                                              rhs=h[:, kc, :], start=(kc == 0), stop=(kc == DC - 1)),
                                     reads=[wtb, hb], writes=[bk[j][1]])
                    o, ob = ost[oi % 2], ostb[oi % 2]
                    oi += 1
                    for j in range(nf):
                        m = min(128, nc_ - j * 128)
                        if j % 2 == 0:
                            S.op("act", lambda e: e.activation(out=o[0:m, j, :], in_=bk[j][0][0:m, :], func=AF.Copy),
                                 reads=[bk[j][1]], writes=[ob])
                        else:
                            S.op("dve", lambda e: e.tensor_copy(out=o[0:m, j, :], in_=bk[j][0][0:m, :]),
                                 reads=[bk[j][1]], writes=[ob])
                    if nc_ % 128 == 0:
                        dst = proj_fm[frow + g0:frow + g0 + nc_, t0:t0 + TT].rearrange("(a p) t -> p a t", p=128)
                        S.dma("pool", dst, o[:, 0:nf, :], reads=[ob], writes=[fm_bufs[ti]])
                    else:
                        S.dma("pool", proj_fm[frow + g0:frow + g0 + nc_, t0:t0 + TT], o[0:nc_, 0, :], reads=[ob],
                              writes=[fm_bufs[ti]])
                frow += n if n != 96 else 128


RMS_EPS = 1e-5


def bcast_row(cx, es, name, vec_ap, n):
    t = cx.sb(es, name, [128, n], F32)
    b = Buf()
    cx.S.dma("sp", t[:], vec_ap.partition_broadcast(128), writes=[b])
    return t, b


def ssd_stage(cx, S_len, proj_tm, tm_bufs, proj_fm, fm_bufs, conv_w, conv_b, dt_bias, a_log, d_skip, norm_w,
              xbc_c, obT, ob_bufs, ident, ones, U):
    nc, S = cx.nc, cx.S
    NCC = CONV_CH // 128
    allfm = list(fm_bufs)
    alltm = list(tm_bufs)
    cbuf = Buf("xbc_c")
    with cx.scope() as es:
        wk = []
        for k in range(4):
            t, b = load_vec_fm(cx, es, "cw%d" % k, conv_w[k, :], CONV_CH, ident)
            wk.append(t)
        cbT, _ = load_vec_fm(cx, es, "cb", conv_b, CONV_CH, ident)
        cvb = Buf()
        S.barrier()
        xin = [cx.sb(es, "xin", [128, 3 + S_len], F32) for _ in range(2)]
        xinb = [Buf() for _ in range(2)]
        xo = [cx.sb(es, "xo", [128, S_len], F32) for _ in range(2)]
        xob = [Buf() for _ in range(2)]
        for cc in range(NCC):
            xi, xib = xin[cc % 2], xinb[cc % 2]
            o, ob = xo[cc % 2], xob[cc % 2]
            S.op("dve", lambda e: e.memset(xi[:, 0:3], 0.0), writes=[xib])
            S.dma("pool", xi[:, 3:3 + S_len], proj_fm[FM_XBC + cc * 128:FM_XBC + (cc + 1) * 128, :], reads=allfm, writes=[xib])
            for t0 in range(0, S_len, TT):
                S.op("dve", lambda e: e.tensor_scalar(out=o[:, t0:t0 + TT], in0=xi[:, 3 + t0:3 + t0 + TT],
                                                      scalar1=wk[3][:, cc:cc + 1], scalar2=cbT[:, cc:cc + 1],
                                                      op0=ALU.mult, op1=ALU.add), reads=[xib], writes=[ob])
                for k in range(3):
                    S.op("dve", lambda e: e.scalar_tensor_tensor(out=o[:, t0:t0 + TT], in0=xi[:, k + t0:k + t0 + TT],
                                                                 scalar=wk[k][:, cc:cc + 1], in1=o[:, t0:t0 + TT],
                                                                 op0=ALU.mult, op1=ALU.add), reads=[xib, ob], writes=[ob])
                S.op("act", lambda e: e.activation(out=o[:, t0:t0 + TT], in_=o[:, t0:t0 + TT], func=AF.Silu),
                     reads=[ob], writes=[ob])
            S.dma("pool", xbc_c[cc * 128:(cc + 1) * 128, :], o[:], reads=[ob], writes=[cbuf])
    NCH = S_len // 128
    xc_v = xbc_c[0:D, :].rearrange("(c p) t -> p c t", p=128)
    b_v = xbc_c[D:D + 1024, :].rearrange("(g p) t -> p g t", p=128)
    c_v = xbc_c[D + 1024:D + 2048, :].rearrange("(g p) t -> p g t", p=128)
    ob_v = obT.rearrange("(c p) t -> p c t", p=128)
    with cx.scope() as es:
        dtb, _ = bcast_row(cx, es, "dtb", dt_bias, SSD_H)
        Arep, _ = bcast_row(cx, es, "Arep", a_log, SSD_H)
        drep, _ = bcast_row(cx, es, "drep", d_skip, SSD_H)
        nwrep, _ = bcast_row(cx, es, "nwrep", norm_w, D)
        cst = Buf()
        S.barrier()
        S.op("act", lambda e: e.activation(out=Arep[:], in_=Arep[:], func=AF.Exp), writes=[cst])
        S.op("dve", lambda e: e.tensor_scalar(out=Arep[:], in0=Arep[:], scalar1=-1.0, scalar2=None, op0=ALU.mult),
             reads=[cst], writes=[cst])
        xT = cx.sb(es, "xT", [128, DC, 128], F32); xTb = Buf()
        BT = cx.sb(es, "BT", [128, 8, 128], F32); BTb = Buf()
        CT = cx.sb(es, "CT", [128, 8, 128], F32); CTb = Buf()
        BTh = cx.sb(es, "BTh", [128, 8, 128], BF16)
        CTh = cx.sb(es, "CTh", [128, 8, 128], BF16)
        Btm = cx.sb(es, "Btm", [128, 8, 128], BF16); Btmb = Buf()
        ztm = cx.sb(es, "ztm", [128, D], F32); zb = Buf()
        xtm = cx.sb(es, "xtm", [128, D], F32); xtb = Buf()
        xdt = cx.sb(es, "xdt", [128, D], BF16); xdb = Buf()
        ysb = cx.sb(es, "ysb", [128, D], F32); yb = Buf()
        state = cx.sb(es, "state", [128, D], F32); stb = Buf()
        stbf = cx.sb(es, "stbf", [128, D], BF16); sbb = Buf()
        obt = cx.sb(es, "obt", [128, DC, 128], BF16); obb = Buf()
        dt = cx.sb(es, "dt", [128, 64], F32); dtbuf = Buf()
        da = cx.sb(es, "da", [128, 64], F32)
        t64 = cx.sb(es, "t64", [128, 64], F32)
        cum = cx.sb(es, "cum", [128, 64], F32); cumb = Buf()
        dend = cx.sb(es, "dend", [128, 64], F32)
        etot = cx.sb(es, "etot", [128, 64], F32); eb = Buf()
        cumT = cx.sb(es, "cumT", [64, 128], F32); cTb = Buf()
        sm = cx.sb(es, "sm", [128, 128], F32); smb = Buf()
        xw = cx.sb(es, "xw", [128, 512], BF16); xwb = Buf()
        Gs = [cx.sb(es, "G", [128, 128], F32) for _ in range(2)]; Gb = [Buf() for _ in range(2)]
        Es = [cx.sb(es, "E", [128, 128], F32) for _ in range(2)]; Eb = [Buf() for _ in range(2)]
        MT = [cx.sb(es, "MT", [128, 128], BF16) for _ in range(2)]; MTb = [Buf() for _ in range(2)]
        Cs = [cx.sb(es, "Cs", [128, 128], BF16) for _ in range(2)]; Csb = [Buf() for _ in range(2)]
        t512 = cx.sb(es, "t512", [128, 512], F32); t5b = Buf()
        selT = [cx.sb(es, "selT", [64, 128], F32) for _ in range(2)]; selb = [Buf() for _ in range(2)]
        ssq = cx.sb(es, "ssq", [128, 8], F32); ssb = Buf()
        S.op("dve", lambda e: e.memset(state[:], 0.0), writes=[stb])
        S.op("dve", lambda e: e.memset(stbf[:], 0.0), writes=[sbb])
        hi = 0
        for ci in range(NCH):
            t0 = ci * 128
            S.dma("pool", xT[:], xc_v[:, :, t0:t0 + 128], reads=[cbuf], writes=[xTb])
            S.dma("pool", BT[:], b_v[:, :, t0:t0 + 128], reads=[cbuf], writes=[BTb])
            S.dma("pool", CT[:], c_v[:, :, t0:t0 + 128], reads=[cbuf], writes=[CTb])
            S.dma("pool", ztm[:], proj_tm[t0:t0 + 128, TM_Z:TM_Z + D], reads=alltm, writes=[zb])
            S.dma("pool", dt[:], proj_tm[t0:t0 + 128, TM_DT:TM_DT + 64], reads=alltm, writes=[dtbuf])
            S.op("dve", lambda e: e.tensor_tensor(out=dt[:], in0=dt[:], in1=dtb[:], op=ALU.add), reads=[dtbuf], writes=[dtbuf])
            S.op("act", lambda e: e.activation(out=t64[:], in_=dt[:], func=AF.Abs), reads=[dtbuf], writes=[dtbuf])
            S.op("act", lambda e: e.activation(out=t64[:], in_=t64[:], func=AF.Exp, scale=-1.0), reads=[dtbuf], writes=[dtbuf])
            S.op("act", lambda e: e.activation(out=t64[:], in_=t64[:], func=AF.Ln, bias=1.0), reads=[dtbuf], writes=[dtbuf])
            S.op("dve", lambda e: e.scalar_tensor_tensor(out=dt[:], in0=dt[:], scalar=0.0, in1=t64[:], op0=ALU.max, op1=ALU.add),
                 reads=[dtbuf], writes=[dtbuf])
            S.op("dve", lambda e: e.tensor_tensor(out=da[:], in0=dt[:], in1=Arep[:], op=ALU.mult), reads=[dtbuf, cst], writes=[dtbuf])
            p1, p1b = cx.bank()
            S.op("pe", lambda e: e.matmul(p1[:, 0:64], lhsT=U[:], rhs=da[:], start=True, stop=True), reads=[dtbuf], writes=[p1b])
            S.op("pe", lambda e: e.matmul(p1[:, 64:128], lhsT=ones[:], rhs=da[:], start=True, stop=True), reads=[dtbuf], writes=[p1b])
            S.op("dve", lambda e: e.tensor_copy(out=cum[:], in_=p1[:, 0:64]), reads=[p1b], writes=[cumb])
            S.op("dve", lambda e: e.tensor_tensor(out=dend[:], in0=p1[:, 64:128], in1=cum[:], op=ALU.subtract), reads=[p1b, cumb], writes=[eb])
            S.op("act", lambda e: e.activation(out=dend[:], in_=dend[:], func=AF.Exp), reads=[eb], writes=[eb])
            S.op("act", lambda e: e.activation(out=etot[:], in_=p1[:, 64:128], func=AF.Exp), reads=[p1b], writes=[eb])
            p2, p2b = cx.bank()
            S.op("pe", lambda e: e.transpose(p2[0:64, 0:128], cum[:], ident[:]), reads=[cumb], writes=[p2b])
            S.op("dve", lambda e: e.tensor_copy(out=cumT[:], in_=p2[0:64, 0:128]), reads=[p2b], writes=[cTb])
            S.op("pool", lambda e: e.tensor_copy(out=BTh[:], in_=BT[:]), reads=[BTb], writes=[BTb])
            S.op("pool", lambda e: e.tensor_copy(out=CTh[:], in_=CT[:]), reads=[CTb], writes=[CTb])
            for q4 in range(2):
                pt, ptb = cx.bank()
                for j in range(4):
                    g = q4 * 4 + j
                    S.op("pe", lambda e: e.transpose(pt[:, j * 128:(j + 1) * 128], BT[:, g, :], ident[:]), reads=[BTb], writes=[ptb])
                S.op("act", lambda e: e.activation(out=Btm[:, q4 * 4:(q4 + 1) * 4, :],
                                                   in_=pt[:].rearrange("p (j n) -> p j n", j=4), func=AF.Copy),
                     reads=[ptb], writes=[Btmb])
            for q4 in range(DC // 4):
                pt, ptb = cx.bank()
                for j in range(4):
                    c = q4 * 4 + j
                    S.op("pe", lambda e: e.transpose(pt[:, j * 128:(j + 1) * 128], xT[:, c, :], ident[:]), reads=[xTb], writes=[ptb])
                S.op("act", lambda e: e.activation(out=xtm[:, q4 * 512:(q4 + 1) * 512], in_=pt[:], func=AF.Copy),
                     reads=[ptb], writes=[xtb])
            S.op("dve", lambda e: e.tensor_tensor(out=xdt[:].rearrange("p (h q) -> p h q", q=64),
                                                  in0=xtm[:].rearrange("p (h q) -> p h q", q=64),
                                                  in1=dt[:].unsqueeze(2).to_broadcast([128, 64, 64]), op=ALU.mult),
                 reads=[xtb, dtbuf], writes=[xdb])
            for g in range(SSD_G):
                gs = slice(g * 512, (g + 1) * 512)
                ps_, psb = cx.bank()
                S.op("pe", lambda e: e.matmul(ps_[:, 0:128], lhsT=BTh[:, g, :], rhs=CTh[:, g, :], start=True, stop=True),
                     reads=[BTb, CTb], writes=[psb])
                S.op("dve", lambda e: e.tensor_tensor(out=sm[:], in0=ps_[:, 0:128], in1=U[:], op=ALU.mult), reads=[psb], writes=[smb])
                S.op("dve", lambda e: e.tensor_tensor(out=xw[:].rearrange("p (h q) -> p h q", q=64),
                                                      in0=xdt[:, gs].rearrange("p (h q) -> p h q", q=64),
                                                      in1=dend[:, g * 8:(g + 1) * 8].unsqueeze(2).to_broadcast([128, 8, 64]),
                                                      op=ALU.mult), reads=[xdb, eb], writes=[xwb])
                ykp = cx.bank(True)
                yk, ykb = ykp
                prs = [cx.bank(True), cx.bank(True)]
                for hl in range(8):
                    hh = g * 8 + hl
                    k = hi % 2
                    hi += 1
                    pr, prb = prs[k]
                    S.op("dve", lambda e: e.tensor_scalar(out=selT[k][:], in0=cumT[:], scalar1=ident[0:64, hh:hh + 1], scalar2=None,
                                                          op0=ALU.mult), reads=[cTb], writes=[selb[k]])
                    S.op("pe", lambda e: e.matmul(pr[:, 0:128], lhsT=ones[0:64, :], rhs=selT[k][:], start=True, stop=True),
                         reads=[selb[k]], writes=[prb])
                    S.op("dve", lambda e: e.tensor_scalar(out=Gs[k][:], in0=pr[:, 0:128], scalar1=cum[:, hh:hh + 1], scalar2=0.0,
                                                          op0=ALU.subtract, op1=ALU.min), reads=[prb, cumb], writes=[Gb[k]])
                    S.op("act", lambda e: e.activation(out=Gs[k][:], in_=Gs[k][:], func=AF.Exp), reads=[Gb[k]], writes=[Gb[k]])
                    S.op("dve", lambda e: e.tensor_tensor(out=MT[k][:], in0=Gs[k][:], in1=sm[:], op=ALU.mult),
                         reads=[Gb[k], smb], writes=[MTb[k]])
                    S.op("act", lambda e: e.activation(out=Es[k][:], in_=pr[:, 0:128], func=AF.Exp), reads=[prb], writes=[Eb[k]])
                    S.op("dve", lambda e: e.tensor_tensor(out=Cs[k][:], in0=Es[k][:], in1=CT[:, g, :], op=ALU.mult),
                         reads=[Eb[k], CTb], writes=[Csb[k]])
                    S.op("pe", lambda e: e.matmul(yk[:, hl * 64:(hl + 1) * 64], lhsT=MT[k][:], rhs=xdt[:, hh * 64:(hh + 1) * 64],
                                                  start=True, stop=False), reads=[MTb[k], xdb], writes=[ykb])
                    S.op("pe", lambda e: e.matmul(yk[:, hl * 64:(hl + 1) * 64], lhsT=Cs[k][:], rhs=stbf[:, hh * 64:(hh + 1) * 64],
                                                  start=False, stop=True), reads=[Csb[k], sbb], writes=[ykb])
                S.op("dve", lambda e: e.tensor_tensor(out=t512[:].rearrange("p (h q) -> p h q", q=64),
                                                      in0=xtm[:, gs].rearrange("p (h q) -> p h q", q=64),
                                                      in1=drep[:, g * 8:(g + 1) * 8].unsqueeze(2).to_broadcast([128, 8, 64]),
                                                      op=ALU.mult), reads=[xtb], writes=[t5b])
                S.op("dve", lambda e: e.tensor_tensor(out=ysb[:, gs], in0=t512[:], in1=yk[:], op=ALU.add),
                     reads=[t5b, ykb], writes=[yb])
                cx.release(ykp, prs[0], prs[1])
                pu, pub = cx.bank()
                S.op("pe", lambda e: e.matmul(pu[:], lhsT=Btm[:, g, :], rhs=xw[:], start=True, stop=True),
                     reads=[Btmb, xwb], writes=[pub])
                S.op("dve", lambda e: e.tensor_tensor(out=state[:, gs].rearrange("p (h q) -> p h q", q=64),
                                                      in0=state[:, gs].rearrange("p (h q) -> p h q", q=64),
                                                      in1=etot[:, g * 8:(g + 1) * 8].unsqueeze(2).to_broadcast([128, 8, 64]),
                                                      op=ALU.mult), reads=[stb, eb, sbb], writes=[stb])
                S.op("dve", lambda e: e.tensor_tensor(out=state[:, gs], in0=state[:, gs], in1=pu[:], op=ALU.add),
                     reads=[stb, pub], writes=[stb])
                S.op("act", lambda e: e.activation(out=stbf[:, gs], in_=state[:, gs], func=AF.Copy), reads=[stb], writes=[sbb])
            S.op("act", lambda e: e.activation(out=ztm[:], in_=ztm[:], func=AF.Silu), reads=[zb], writes=[zb])
            S.op("dve", lambda e: e.tensor_tensor(out=ysb[:], in0=ysb[:], in1=ztm[:], op=ALU.mult), reads=[yb, zb], writes=[yb])
            for g in range(SSD_G):
                gs = slice(g * 512, (g + 1) * 512)
                S.op("act", lambda e: e.activation(out=t512[:], in_=ysb[:, gs], func=AF.Square, accum_out=ssq[:, g:g + 1]),
                     reads=[yb], writes=[t5b, ssb])
            S.op("dve", lambda e: e.tensor_scalar(out=ssq[:], in0=ssq[:], scalar1=1.0 / 512, scalar2=float(RMS_EPS),
                                                  op0=ALU.mult, op1=ALU.add), reads=[ssb], writes=[ssb])
            S.op("act", lambda e: e.activation(out=ssq[:], in_=ssq[:], func=AF.Sqrt), reads=[ssb], writes=[ssb])
            S.op("dve", lambda e: e.reciprocal(out=ssq[:], in_=ssq[:]), reads=[ssb], writes=[ssb])
            S.op("dve", lambda e: e.tensor_tensor(out=ysb[:].rearrange("p (g q) -> p g q", q=512),
                                                  in0=ysb[:].rearrange("p (g q) -> p g q", q=512),
                                                  in1=ssq[:].unsqueeze(2).to_broadcast([128, 8, 512]), op=ALU.mult),
                 reads=[yb, ssb], writes=[yb])
            S.op("dve", lambda e: e.tensor_tensor(out=ysb[:], in0=ysb[:], in1=nwrep[:], op=ALU.mult), reads=[yb], writes=[yb])
            for q4 in range(DC // 4):
                pt, ptb = cx.bank()
                for j in range(4):
                    c = q4 * 4 + j
                    S.op("pe", lambda e: e.transpose(pt[:, j * 128:(j + 1) * 128], ysb[:, c * 128:(c + 1) * 128], ident[:]),
                         reads=[yb], writes=[ptb])
                S.op("act", lambda e: e.activation(out=obt[:, q4 * 4:(q4 + 1) * 4, :],
                                                   in_=pt[:].rearrange("p (j n) -> p j n", j=4), func=AF.Copy),
                     reads=[ptb], writes=[obb])
            S.dma("pool", ob_v[:, :, t0:t0 + 128], obt[:], reads=[obb], writes=[ob_bufs[ci // 4]])
```
